# Optimizing a Trainium2 kernel written in Bass

```python
import jax, jax.numpy as jnp
from jax import lax
import numpy as np

D_MODEL = 1024
BATCH = 32
SEQ = 2048
DEPTH = 4

CTX_LEN = 256
GRID_W = 64
N_BRANCH = 3
NA_HEADS = 8
NA_DH = 64
NA_KH = 8
NA_KW = 16
MLA_HEADS = 8
MLA_NOPE = 64
MLA_ROPE = 32
MLA_V = 64
MLA_Q_RANK = 384
MLA_KV_RANK = 256
MLA_QBLOCK = 128
LRU_W = 512
LRU_BLOCKS = 8
LRU_BS = LRU_W // LRU_BLOCKS
LRU_C = 8.0
CONV_W = 4
D_FF = 4 * D_MODEL
ROPE_BASE = 10000.0
EPS = 1e-6
NEG = -1e30

NA_WIDTH = NA_HEADS * NA_DH
MLA_WIDTH = MLA_HEADS * MLA_V
MLA_QK = MLA_NOPE + MLA_ROPE
SPLITS = (NA_WIDTH, NA_WIDTH, NA_WIDTH, MLA_Q_RANK, MLA_KV_RANK, MLA_ROPE, LRU_W, LRU_W, N_BRANCH * D_MODEL)
IN_COLS = sum(SPLITS)

kernel_name = "hybrid_na_mla_rglru_prefix_block"


def rmsnorm(x, g):
    xf = x.astype(jnp.float32)
    y = xf * lax.rsqrt(jnp.mean(xf * xf, axis=-1, keepdims=True) + EPS)
    return (y * g.astype(jnp.float32)).astype(x.dtype)


def split_cols(p):
    idx = np.cumsum(np.array(SPLITS))[:-1].tolist()
    return jnp.split(p, idx, axis=-1)


def attend(q, k, v, scale):
    s = jnp.einsum('bqhd,bkhd->bhqk', q, k).astype(jnp.float32) * scale
    p = jax.nn.softmax(s, axis=-1).astype(v.dtype)
    return jnp.einsum('bhqk,bkhd->bqhd', p, v)


def axial_rope(z):
    T = z.shape[1]
    t = jnp.arange(T)
    row = (t // GRID_W).astype(jnp.float32)
    col = (t % GRID_W).astype(jnp.float32)
    half = z.shape[-1] // 2
    nf = half // 2
    inv = ROPE_BASE ** (-jnp.arange(nf, dtype=jnp.float32) / nf)

    def rot(u, pos):
        ang = pos[:, None] * inv
        cos = jnp.cos(ang)[None, :, None, :]
        sin = jnp.sin(ang)[None, :, None, :]
        uf = u.astype(jnp.float32)
        u1, u2 = uf[..., :nf], uf[..., nf:]
        return jnp.concatenate([u1 * cos - u2 * sin, u1 * sin + u2 * cos], axis=-1).astype(u.dtype)

    return jnp.concatenate([rot(z[..., :half], row), rot(z[..., half:], col)], axis=-1)


def rope_tail(z):
    return jnp.concatenate([z[..., :MLA_NOPE], axial_rope(z[..., MLA_NOPE:])], axis=-1)


def na_latent(q, k, v, kc, vc, rpb):
    B, S, H, dh = q.shape
    rows = S // GRID_W
    kh = min(NA_KH, rows)
    ncb = GRID_W // NA_KW
    scale = dh ** -0.5
    qg = q.reshape(B, rows, GRID_W, H, dh)
    kg = k.reshape(B, rows, GRID_W, H, dh)
    vg = v.reshape(B, rows, GRID_W, H, dh)
    col_q = jnp.arange(GRID_W).reshape(ncb, NA_KW)
    blk_start = jnp.clip(jnp.arange(ncb) * NA_KW - NA_KW // 2, 0, GRID_W - 2 * NA_KW)
    key_cols = blk_start[:, None] + jnp.arange(2 * NA_KW)
    cs = jnp.clip(col_q - NA_KW // 2, 0, GRID_W - NA_KW)
    kcol = key_cols[:, None, :]
    col_valid = (kcol >= cs[..., None]) & (kcol < cs[..., None] + NA_KW)
    dc_idx = jnp.clip(kcol - col_q[..., None] + NA_KW - 1, 0, 2 * NA_KW - 2)
    n_win = kh * 2 * NA_KW

    def row_fn(r):
        rs = jnp.clip(r - kh // 2, 0, rows - kh)
        qr = lax.dynamic_index_in_dim(qg, r, axis=1, keepdims=False).reshape(B, ncb, NA_KW, H, dh)
        kr = lax.dynamic_slice_in_dim(kg, rs, kh, axis=1)[:, :, key_cols]
        vr = lax.dynamic_slice_in_dim(vg, rs, kh, axis=1)[:, :, key_cols]
        s_win = jnp.einsum('bnqhd,bmnjhd->bhnqmj', qr, kr).astype(jnp.float32) * scale
        dr_idx = rs + jnp.arange(kh) - r + NA_KH - 1
        bias = rpb[:, dr_idx[None, None, :, None], dc_idx[:, :, None, :]]
        s_win = jnp.where(col_valid[:, :, None, :], s_win + bias.astype(jnp.float32), NEG)
        s_ctx = jnp.einsum('bnqhd,blhd->bhnql', qr, kc).astype(jnp.float32) * scale
        s = jnp.concatenate([s_win.reshape(B, H, ncb, NA_KW, n_win), s_ctx], axis=-1)
        p = jax.nn.softmax(s, axis=-1).astype(v.dtype)
        p_win = p[..., :n_win].reshape(B, H, ncb, NA_KW, kh, 2 * NA_KW)
        p_ctx = p[..., n_win:]
        o = jnp.einsum('bhnqmj,bmnjhd->bnqhd', p_win, vr) + jnp.einsum('bhnql,blhd->bnqhd', p_ctx, vc)
        return o.reshape(B, GRID_W, H * dh)

    out = lax.map(row_fn, jnp.arange(rows))
    return jnp.transpose(out, (1, 0, 2, 3)).reshape(B, S, H * dh)


def mla_q(pq, qa_g, w_qb, q_g):
    B, T, _ = pq.shape
    q = (rmsnorm(pq, qa_g) @ w_qb).reshape(B, T, MLA_HEADS, MLA_QK)
    return rmsnorm(q, q_g)


def mla_kv(pkv, pr, kva_g, w_kvb, k_g):
    B, T, _ = pkv.shape
    kv = (rmsnorm(pkv, kva_g) @ w_kvb).reshape(B, T, MLA_HEADS, MLA_NOPE + MLA_V)
    k_nope, v = kv[..., :MLA_NOPE], kv[..., MLA_NOPE:]
    k_r = jnp.broadcast_to(pr[:, :, None, :], (B, T, MLA_HEADS, MLA_ROPE))
    k = rmsnorm(jnp.concatenate([k_nope, k_r], axis=-1), k_g)
    return k, v


def mla_latent(q, k, v, kc, vc):
    B, S, H, dq = q.shape
    kall = jnp.concatenate([k, kc], axis=1)
    vall = jnp.concatenate([v, vc], axis=1)
    nb = S // MLA_QBLOCK
    qb = q.reshape(B, nb, MLA_QBLOCK, H, dq).swapaxes(0, 1)
    o = lax.map(lambda qi: attend(qi, kall, vall, dq ** -0.5), qb)
    return o.swapaxes(0, 1).reshape(B, S, H * MLA_V)


def conv_centred(u, w, b):
    T = u.shape[1]
    left = CONV_W // 2
    up = jnp.pad(u, ((0, 0), (left, CONV_W - 1 - left), (0, 0)))
    return b + sum(up[:, j:j + T] * w[j] for j in range(CONV_W))


def rglru_ab(u, wa, ba, wx, bx, lam):
    B, T, W = u.shape
    uf = u.astype(jnp.float32)
    ub = uf.reshape(B, T, LRU_BLOCKS, LRU_BS)
    r = jax.nn.sigmoid(jnp.einsum('btnc,ncd->btnd', ub, wa.astype(jnp.float32)).reshape(B, T, W) + ba.astype(jnp.float32))
    ig = jax.nn.sigmoid(jnp.einsum('btnc,ncd->btnd', ub, wx.astype(jnp.float32)).reshape(B, T, W) + bx.astype(jnp.float32))
    log_a = -LRU_C * r * jax.nn.softplus(-lam.astype(jnp.float32))
    a = jnp.exp(log_a)
    bterm = jnp.sqrt(-jnp.expm1(2.0 * log_a)) * ig * uf
    return a, bterm


def lin_scan(a, b, h0, reverse):
    if h0 is not None:
        idx = -1 if reverse else 0
        b = b.at[:, idx].add(a[:, idx] * h0)

    def comb(l, r):
        al, bl = l
        ar, br = r
        return al * ar, ar * bl + br

    _, h = lax.associative_scan(comb, (a, b), reverse=reverse, axis=1)
    return h


def token_mixers(hx, hc, w_in, na_qg, na_kg, rpb, qa_g, w_qb, kva_g, w_kvb, q_g, k_g,
                 conv_w, conv_b, wa, ba, wx, bx, lam, w_na_o, w_mla_o, w_lru_o, w_o, need_ctx):
    B, S, _ = hx.shape
    L = hc.shape[1]
    nqx, nkx, nvx, mqx, mkvx, mrx, lux, lgx, gtx = split_cols(hx @ w_in)
    nqc, nkc, nvc, mqc, mkvc, mrc, luc, lgc, gtc = split_cols(hc @ w_in)
    heads = lambda p: p.reshape(p.shape[0], p.shape[1], NA_HEADS, NA_DH)

    qx = rmsnorm(heads(nqx), na_qg)
    kx = rmsnorm(heads(nkx), na_kg)
    vx = heads(nvx)
    kc = rmsnorm(heads(nkc), na_kg)
    vc = heads(nvc)
    o_na_x = na_latent(qx, kx, vx, kc, vc, rpb)

    q_mx = rope_tail(mla_q(mqx, qa_g, w_qb, q_g))
    k_mx, v_mx = mla_kv(mkvx, mrx, kva_g, w_kvb, k_g)
    k_mx = rope_tail(k_mx)
    k_mc, v_mc = mla_kv(mkvc, mrc, kva_g, w_kvb, k_g)
    o_mla_x = mla_latent(q_mx, k_mx, v_mx, k_mc, v_mc)

    ux = conv_centred(lux, conv_w, conv_b)
    uc = conv_centred(luc, conv_w, conv_b)
    hsum_x = None
    hsum_c = None
    for d, rev in ((0, False), (1, True)):
        a_c, b_c = rglru_ab(uc, wa[d], ba[d], wx[d], bx[d], lam[d])
        h_c = lin_scan(a_c, b_c, None, rev)
        a_x, b_x = rglru_ab(ux, wa[d], ba[d], wx[d], bx[d], lam[d])
        h_x = lin_scan(a_x, b_x, h_c[:, 0 if rev else -1], rev)
        hsum_x = h_x if hsum_x is None else hsum_x + h_x
        if need_ctx:
            hsum_c = h_c if hsum_c is None else hsum_c + h_c
    o_lru_x = jax.nn.gelu(lgx) * hsum_x.astype(lgx.dtype)

    g_na, g_mla, g_lru = jnp.split(jax.nn.sigmoid(gtx), N_BRANCH, axis=-1)
    yx = (g_na * (o_na_x @ w_na_o) + g_mla * (o_mla_x @ w_mla_o) + g_lru * (o_lru_x @ w_lru_o)) @ w_o
    if not need_ctx:
        return yx, None

    qc = rmsnorm(heads(nqc), na_qg)
    o_na_c = attend(qc, kc, vc, NA_DH ** -0.5).reshape(B, L, NA_WIDTH)
    q_mc = mla_q(mqc, qa_g, w_qb, q_g)
    o_mla_c = attend(q_mc, k_mc, v_mc, MLA_QK ** -0.5).reshape(B, L, MLA_WIDTH)
    o_lru_c = jax.nn.gelu(lgc) * hsum_c.astype(lgc.dtype)
    gc_na, gc_mla, gc_lru = jnp.split(jax.nn.sigmoid(gtc), N_BRANCH, axis=-1)
    yc = (gc_na * (o_na_c @ w_na_o) + gc_mla * (o_mla_c @ w_mla_o) + gc_lru * (o_lru_c @ w_lru_o)) @ w_o
    return yx, yc


def sq_relu_mlp(h, w1, w2):
    return jnp.square(jax.nn.relu(h @ w1)) @ w2


def setup_inputs(seed: int = 0) -> dict:
    key = jax.random.key(seed)
    ks = list(jax.random.split(key, 40))
    cnt = [0]

    def nrm(shape, s):
        k = ks[cnt[0]]
        cnt[0] += 1
        return jax.random.normal(k, shape, jnp.float32) * s

    Dp = DEPTH
    a_c = jax.random.uniform(ks[39], (Dp, 2, LRU_W), jnp.float32, 0.9, 0.999)
    a0 = a_c ** (1.0 / LRU_C)
    lam = jnp.log(a0) - jnp.log1p(-a0)
    return {
        "x": nrm((BATCH, SEQ, D_MODEL), 1.0),
        "c": nrm((BATCH, D_MODEL), 1.0),
        "ctx": nrm((BATCH, CTX_LEN, D_MODEL), 1.0),
        "c_ctx": nrm((D_MODEL,), 1.0),
        "w_mod": nrm((Dp, D_MODEL, 6 * D_MODEL), 0.5 * D_MODEL ** -0.5),
        "b_mod": nrm((Dp, 6 * D_MODEL), 0.02),
        "g_mix": 1.0 + nrm((Dp, D_MODEL), 0.02),
        "g_mlp": 1.0 + nrm((Dp, D_MODEL), 0.02),
        "w_in": nrm((Dp, D_MODEL, IN_COLS), D_MODEL ** -0.5),
        "na_q_gain": 1.0 + nrm((Dp, NA_DH), 0.02),
        "na_k_gain": 1.0 + nrm((Dp, NA_DH), 0.02),
        "na_rpb": nrm((Dp, NA_HEADS, 2 * NA_KH - 1, 2 * NA_KW - 1), 0.1),
        "mla_qa_gain": 1.0 + nrm((Dp, MLA_Q_RANK), 0.02),
        "w_q_b": nrm((Dp, MLA_Q_RANK, MLA_HEADS * MLA_QK), MLA_Q_RANK ** -0.5),
        "mla_kva_gain": 1.0 + nrm((Dp, MLA_KV_RANK), 0.02),
        "w_kv_b": nrm((Dp, MLA_KV_RANK, MLA_HEADS * (MLA_NOPE + MLA_V)), MLA_KV_RANK ** -0.5),
        "mla_q_gain": 1.0 + nrm((Dp, MLA_QK), 0.02),
        "mla_k_gain": 1.0 + nrm((Dp, MLA_QK), 0.02),
        "lru_conv_w": nrm((Dp, CONV_W, LRU_W), CONV_W ** -0.5),
        "lru_conv_b": nrm((Dp, LRU_W), 0.02),
        "lru_wa": nrm((Dp, 2, LRU_BLOCKS, LRU_BS, LRU_BS), LRU_BS ** -0.5),
        "lru_ba": nrm((Dp, 2, LRU_W), 0.02),
        "lru_wx": nrm((Dp, 2, LRU_BLOCKS, LRU_BS, LRU_BS), LRU_BS ** -0.5),
        "lru_bx": nrm((Dp, 2, LRU_W), 0.02),
        "lru_lambda": lam,
        "w_na_o": nrm((Dp, NA_WIDTH, D_MODEL), NA_WIDTH ** -0.5),
        "w_mla_o": nrm((Dp, MLA_WIDTH, D_MODEL), MLA_WIDTH ** -0.5),
        "w_lru_o": nrm((Dp, LRU_W, D_MODEL), LRU_W ** -0.5),
        "w_o": nrm((Dp, D_MODEL, D_MODEL), D_MODEL ** -0.5),
        "w_ff1": nrm((Dp, D_MODEL, D_FF), D_MODEL ** -0.5),
        "w_ff2": nrm((Dp, D_FF, D_MODEL), D_FF ** -0.5),
    }


def reference(x, c, ctx, c_ctx, w_mod, b_mod, g_mix, g_mlp, w_in, na_q_gain, na_k_gain, na_rpb,
              mla_qa_gain, w_q_b, mla_kva_gain, w_kv_b, mla_q_gain, mla_k_gain,
              lru_conv_w, lru_conv_b, lru_wa, lru_ba, lru_wx, lru_bx, lru_lambda,
              w_na_o, w_mla_o, w_lru_o, w_o, w_ff1, w_ff2):
    s_lat = jax.nn.silu(c)
    s_ctx = jax.nn.silu(c_ctx)
    for i in range(DEPTH):
        need_ctx = i < DEPTH - 1
        mx = (s_lat @ w_mod[i] + b_mod[i])[:, None, :]
        mc = s_ctx @ w_mod[i] + b_mod[i]
        sh1x, sc1x, g1x, sh2x, sc2x, g2x = jnp.split(mx, 6, axis=-1)
        sh1c, sc1c, g1c, sh2c, sc2c, g2c = jnp.split(mc, 6, axis=-1)
        hx = rmsnorm(x, g_mix[i]) * (1.0 + sc1x) + sh1x
        hc = rmsnorm(ctx, g_mix[i]) * (1.0 + sc1c) + sh1c
        yx, yc = token_mixers(hx, hc, w_in[i], na_q_gain[i], na_k_gain[i], na_rpb[i],
                              mla_qa_gain[i], w_q_b[i], mla_kva_gain[i], w_kv_b[i], mla_q_gain[i], mla_k_gain[i],
                              lru_conv_w[i], lru_conv_b[i], lru_wa[i], lru_ba[i], lru_wx[i], lru_bx[i], lru_lambda[i],
                              w_na_o[i], w_mla_o[i], w_lru_o[i], w_o[i], need_ctx)
        x = x + g1x * yx
        hx2 = rmsnorm(x, g_mlp[i]) * (1.0 + sc2x) + sh2x
        x = x + g2x * sq_relu_mlp(hx2, w_ff1[i], w_ff2[i])
        if need_ctx:
            ctx = ctx + g1c * yc
            hc2 = rmsnorm(ctx, g_mlp[i]) * (1.0 + sc2c) + sh2c
            ctx = ctx + g2c * sq_relu_mlp(hc2, w_ff1[i], w_ff2[i])
    return x
```

```python
import contextlib
import numpy as np
import concourse.bass as bass
import concourse.mybir as mybir
from concourse.bass_utils import run_bass_kernel_spmd

F32 = mybir.dt.float32
BF16 = mybir.dt.bfloat16
AF = mybir.ActivationFunctionType
ALU = mybir.AluOpType

D = 1024
T = 2048
LC = 256
NT = T + LC
DEPTH = 4
NCORES = 8
BPC = 4
GRID_W = 64
NKT = NT // 128
IN_COLS = 6304
C_NQ, C_NK, C_NV, C_MQ, C_MKV, C_MR, C_LX, C_LG, C_GT = 0, 512, 1024, 1536, 1920, 2176, 2208, 2720, 3232
EPS = 1e-6
NEG = -30000.0
CHUNKS = [(0, 512), (512, 512), (1024, 512), (1536, 512), (2048, 256)]

V_BMOD, V_GMIX, V_GMLP, V_NAQ, V_NAK, V_QA, V_KVA, V_MQ, V_MK, V_CW, V_CB, V_BA, V_BX, V_LAM = (
    0, 48, 56, 64, 65, 66, 69, 71, 72, 73, 89, 93, 101, 109)
NV = 117

ENGS = ("pe", "act", "dve", "pool", "sp")
NDMA_SLOTS = 12


class Op:
    __slots__ = ("eng", "fn", "dma", "deps", "idx", "sig", "cnt", "slot", "val")

    def __init__(self, eng, fn, dma):
        self.eng = eng
        self.fn = fn
        self.dma = dma
        self.deps = set()
        self.sig = False
        self.cnt = 0
        self.slot = -1
        self.val = 0


class Prog:
    def __init__(self, nc, same_engine_sync=True):
        self.nc = nc
        self.ops = []
        self.per_eng = {e: [] for e in ENGS}
        self.res = {}
        self.same_engine_sync = same_engine_sync
        self.pending_barrier = {e: None for e in ENGS}
        self.live_dma = []

    def op(self, eng, fn, r=(), w=(), dma=False):
        o = Op(eng, fn, dma)
        o.idx = len(self.ops)
        deps = o.deps
        res = self.res
        for k in r:
            ent = res.get(k)
            if ent is None:
                ent = res[k] = [None, []]
            if ent[0] is not None:
                deps.add(ent[0])
            ent[1].append(o)
        for k in w:
            ent = res.get(k)
            if ent is None:
                ent = res[k] = [None, []]
            if ent[0] is not None:
                deps.add(ent[0])
            for rd in ent[1]:
                if rd is not o:
                    deps.add(rd)
            ent[0] = o
            ent[1] = []
        deps.discard(o)
        pb = self.pending_barrier[eng]
        if pb is not None:
            deps.update(pb)
            self.pending_barrier[eng] = None
        self.ops.append(o)
        self.per_eng[eng].append(o)
        if dma:
            self.live_dma.append(o)
        return o

    def barrier(self):
        last = []
        for e in ENGS:
            for o in reversed(self.per_eng[e]):
                if not o.dma:
                    last.append(o)
                    break
        last.extend(self.live_dma)
        self.live_dma = []
        for e in ENGS:
            cur = self.pending_barrier[e]
            s = set(last)
            if cur is not None:
                s |= cur
            self.pending_barrier[e] = s

    def emit(self, stack):
        nc = self.nc
        ses = self.same_engine_sync
        for o in self.ops:
            for d in o.deps:
                if d.dma:
                    continue
                if d.eng != o.eng:
                    d.sig = True
                elif o.dma:
                    d.sig = True
                elif ses and d.eng != "pe":
                    d.sig = True
        engsem = {e: stack.enter_context(nc.semaphore("s_" + e)) for e in ENGS}
        dmasem = {e: [stack.enter_context(nc.semaphore("d_%s%d" % (e, i))) for i in range(NDMA_SLOTS)]
                  for e in ("sp", "pool", "act")}
        for e in ENGS:
            c = 0
            for o in self.per_eng[e]:
                if not o.dma and o.sig:
                    c += 1
                    o.cnt = c
        slot_val = {e: [0] * NDMA_SLOTS for e in dmasem}
        slot_prev = {e: [None] * NDMA_SLOTS for e in dmasem}
        nd = {e: 0 for e in dmasem}
        for o in self.ops:
            if o.dma:
                e = o.eng
                s = nd[e] % NDMA_SLOTS
                nd[e] += 1
                prev = slot_prev[e][s]
                if prev is not None:
                    o.deps.add(prev)
                slot_val[e][s] += 16
                o.slot = s
                o.val = slot_val[e][s]
                slot_prev[e][s] = o

        def emit_eng(e, eo):
            seen = {}
            for o in self.per_eng[e]:
                waits = {}
                for d in o.deps:
                    if d.dma:
                        sem = dmasem[d.eng][d.slot]
                        v = d.val
                    else:
                        if not d.sig:
                            continue
                        sem = engsem[d.eng]
                        v = d.cnt
                    key = id(sem)
                    if seen.get(key, 0) >= v:
                        continue
                    if key not in waits or waits[key][1] < v:
                        waits[key] = (sem, v)
                for key, (sem, v) in waits.items():
                    eo.wait_ge(sem, v)
                    seen[key] = v
                ins = o.fn(eo)
                if o.dma:
                    ins.then_inc(dmasem[e][o.slot], 16)
                elif o.sig:
                    ins.then_inc(engsem[e], 1)
            return seen

        block = stack.enter_context(nc.Block())

        @block.tensor
        def _(eo):
            emit_eng("pe", eo)

        @block.scalar
        def _(eo):
            emit_eng("act", eo)

        @block.vector
        def _(eo):
            emit_eng("dve", eo)

        @block.gpsimd
        def _(eo):
            emit_eng("pool", eo)

        @block.sync
        def _(eo):
            seen = emit_eng("sp", eo)
            for e2 in dmasem:
                for s in range(NDMA_SLOTS):
                    v = slot_val[e2][s]
                    if v > 0 and seen.get(id(dmasem[e2][s]), 0) < v:
                        eo.wait_ge(dmasem[e2][s], v)


class Arena:
    def __init__(self, nc, stack, nbytes):
        self.n = nbytes
        self.t = stack.enter_context(nc.sbuf_tensor("arena", [128, nbytes // 4], F32))

    def view(self, off, shape, dtype):
        esz = 2 if dtype == BF16 else 4
        n = 1
        for s in shape:
            n *= s
        nb = n * esz
        assert off % 4 == 0 and nb % 4 == 0, (off, nb)
        assert off + nb <= self.n, ("arena overflow", off, nb, self.n)
        ap = self.t[:, off // 4:(off + nb) // 4]
        if dtype != F32:
            ap = ap.bitcast(dtype)
        if len(shape) == 2:
            ap = ap.rearrange("p (a b) -> p a b", a=shape[0])
        elif len(shape) == 3:
            ap = ap.rearrange("p (a b c) -> p a b c", a=shape[0], b=shape[1])
        elif len(shape) == 4:
            ap = ap.rearrange("p (a b c d) -> p a b c d", a=shape[0], b=shape[1], c=shape[2])
        return ap


class Lay:
    def __init__(self, base, limit):
        self.p = base
        self.limit = limit

    def take(self, nbytes):
        nbytes = (nbytes + 31) // 32 * 32
        o = self.p
        self.p += nbytes
        assert self.p <= self.limit, ("layout overflow", self.p, self.limit)
        return o


def build_program(nseq=BPC, nlayers=DEPTH, depth_total=DEPTH, dbg=None):
    nc = bass.Bass("TRN2", target_bir_lowering=False)
    L = nlayers

    def din(name, shape):
        return nc.dram_tensor(name, list(shape), F32, kind="ExternalInput").ap()

    x_d = din("x", [nseq, T, D])
    ctx_d = din("ctx", [nseq, LC, D])
    ct_d = din("ct", [128, 8 * 5])
    vecs_d = din("vecs", [L, 128, NV])
    tb_d = din("tbsrc", [L, 8, 128, 1024])
    ident_d = din("ident", [128, 128])
    rmat_d = din("rmat", [128, 96])
    shift_d = din("shiftm", [128, 96])
    cmask_d = din("colmask", [128, 1024])
    rope_d = din("ropecs", [2, 128, T])
    w_mod_d = din("w_mod", [L, D, 6 * D])
    w_in_d = din("w_in", [L, D, IN_COLS])
    w_qb_d = din("w_q_b", [L, 384, 768])
    w_kvb_d = din("w_kv_b", [L, 256, 1024])
    wa_d = din("lru_wa", [L, 2, 8, 64, 64])
    wx_d = din("lru_wx", [L, 2, 8, 64, 64])
    w_bo_d = [din("w_na_o", [L, 512, D]), din("w_mla_o", [L, 512, D]), din("w_lru_o", [L, 512, D])]
    w_o_d = din("w_o", [L, D, D])
    w_ff1_d = din("w_ff1", [L, D, 4 * D])
    w_ff2_d = din("w_ff2", [L, 4 * D, D])
    out_d = nc.dram_tensor("out", [nseq, T, D], F32, kind="ExternalOutput").ap()
    dbg_d = None
    if dbg:
        dbg_d = nc.dram_tensor("dbg", [128, dbg["n"]], F32, kind="ExternalOutput").ap()

    stack = contextlib.ExitStack()
    with stack:
        P = Prog(nc)
        ARENA = 212480
        A = Arena(nc, stack, ARENA)
        lay = Lay(0, ARENA)

        def alloc(lay_, shape, dtype):
            n = 1
            for s in shape:
                n *= s
            return A.view(lay_.take(n * (2 if dtype == BF16 else 4)), shape, dtype)

        XT = alloc(lay, [8, NT], F32)
        HT = alloc(lay, [8, NT], BF16)
        IDF = alloc(lay, [128], F32)
        IDB = alloc(lay, [128], BF16)
        ONES = alloc(lay, [128], BF16)
        BD = alloc(lay, [128], BF16)
        RMAT = alloc(lay, [96], BF16)
        SHIFT = alloc(lay, [96], BF16)
        VECS = alloc(lay, [L, NV], F32)
        MODV = alloc(lay, [L, 48, 5], F32)
        GS = alloc(lay, [L, 2, 8, 5], F32)
        CDEC = alloc(lay, [L, 8], F32)
        GQ = alloc(lay, [L, 2], F32)
        CST = alloc(lay, [8], F32)
        PH0 = lay.p
        PHL = ARENA

        ps_tiles = [stack.enter_context(nc.psum_tensor("ps%d" % i, [128, 512], F32)) for i in range(8)]
        ps_ctr = [0]

        def PS():
            i = ps_ctr[0] % 6
            ps_ctr[0] += 1
            return ps_tiles[i], ("ps", i)

        psl_ctr = [0]

        def PSL():
            i = 6 + psl_ctr[0] % 2
            psl_ctr[0] += 1
            return ps_tiles[i], ("ps", i)

        uid = [0]

        def U(name):
            uid[0] += 1
            return (name, uid[0])

        def wsrc(wd, l, r0, nr, c0, ncol):
            return wd[l, r0:r0 + nr, c0:c0 + ncol].rearrange("(kc ki) n -> ki kc n", ki=128)

        def wload(dst, src, key):
            P.op("pool", lambda e: e.dma_start(out=dst, in_=src), w=[key], dma=True)

        def mm(ps_ap, lhsT, rhs, start, stop, r, w):
            P.op("pe", lambda e: e.matmul(ps_ap, lhsT=lhsT, rhs=rhs, start=start, stop=stop), r=r, w=w)

        def act(out, in_, func, r, w, bias=None, scale=None):
            kw = {}
            if bias is not None:
                kw["bias"] = bias
            if scale is not None:
                kw["scale"] = scale
            P.op("act", lambda e: e.activation(out=out, in_=in_, func=func, **kw), r=r, w=w)

        def rstd_from_ps(ps_ap, npart, n, inv_dim, tmp, tmpk, psk):
            act(tmp[0:npart, 0:n], ps_ap, AF.Sqrt, r=[psk, "cst"], w=[tmpk], bias=CST[0:npart, 0:1], scale=inv_dim)
            P.op("dve", lambda e: e.reciprocal(out=tmp[0:npart, 0:n], in_=tmp[0:npart, 0:n]), r=[tmpk], w=[tmpk])

        P.op("sp", lambda e: e.dma_start(out=IDF, in_=ident_d[:, :]), w=["idf"], dma=True)
        P.op("sp", lambda e: e.dma_start(out=VECS, in_=vecs_d.rearrange("l p n -> p l n")), w=["vecs"], dma=True)
        P.op("pool", lambda e: e.dma_start(out=IDB, in_=ident_d[:, :]), w=["idb"], dma=True)
        P.op("pool", lambda e: e.dma_start(out=RMAT, in_=rmat_d[:, :]), w=["rmat"], dma=True)
        P.op("pool", lambda e: e.dma_start(out=SHIFT, in_=shift_d[:, :]), w=["shift"], dma=True)
        P.op("dve", lambda e: e.memset(ONES, 1.0), w=["ones"])
        P.op("dve", lambda e: e.memset(BD, 0.0), w=["bd"])
        P.op("dve", lambda e: e.memset(BD[0:64, 0:64], 1.0), w=["bd"])
        P.op("dve", lambda e: e.memset(BD[64:128, 64:128], 1.0), w=["bd"])
        P.op("dve", lambda e: e.memset(CST[:, 0:1], EPS), w=["cst"])
        P.op("dve", lambda e: e.memset(CST[:, 1:2], 1.0), w=["cst"])
        P.op("dve", lambda e: e.memset(CST[:, 2:3], 0.0), w=["cst"])

        pl = Lay(PH0, PHL)
        CTF = alloc(pl, [8, 5], F32)
        CTB = alloc(pl, [8, 5], BF16)
        WM = [alloc(pl, [8, 1024], BF16) for _ in range(2)]
        P.op("sp", lambda e: e.dma_start(out=CTF, in_=ct_d.rearrange("p (a b) -> p a b", a=8)), w=["ctf"], dma=True)
        act(CTB, CTF, AF.Silu, r=["ctf"], w=["ctb"])
        wmi = 0
        for l in range(L):
            for cb in range(6):
                wb = WM[wmi % 2]
                wk = ("wm", wmi % 2)
                wmi += 1
                wload(wb, wsrc(w_mod_d, l, 0, D, cb * 1024, 1024), wk)
                pst, psk = PS()
                for f in range(8):
                    for kc in range(8):
                        mm(pst[:, f * 5:(f + 1) * 5], wb[:, kc, f * 128:(f + 1) * 128], CTB[:, kc, :],
                           kc == 0, kc == 7, r=[wk, "ctb"], w=[psk])
                for f in range(8):
                    t = cb * 8 + f
                    act(MODV[:, l, t, :], pst[:, f * 5:(f + 1) * 5], AF.Identity, r=[psk, "vecs"], w=["modv"],
                        bias=VECS[:, l, V_BMOD + t:V_BMOD + t + 1])
            for ni, (cbsc, vg) in enumerate(((1, V_GMIX), (4, V_GMLP))):
                for f in range(8):
                    P.op("dve", lambda e, l=l, ni=ni, f=f, cbsc=cbsc, vg=vg: e.tensor_scalar(
                        out=GS[:, l, ni, f, :], in0=MODV[:, l, cbsc * 8 + f, :], scalar1=1.0,
                        scalar2=VECS[:, l, vg + f:vg + f + 1], op0=ALU.add, op1=ALU.mult),
                        r=["modv", "vecs"], w=["gs"])
            act(CDEC[:, l, :], VECS[:, l, V_LAM:V_LAM + 8], AF.Exp, r=["vecs"], w=["cdec"], scale=-1.0)
            act(CDEC[:, l, :], CDEC[:, l, :], AF.Ln, r=["cdec", "cst"], w=["cdec"], bias=CST[:, 1:2])
            P.op("dve", lambda e, l=l: e.tensor_scalar(out=CDEC[:, l, :], in0=CDEC[:, l, :], scalar1=-8.0,
                                                       scalar2=None, op0=ALU.mult), r=["cdec"], w=["cdec"])
            P.op("dve", lambda e, l=l: e.tensor_scalar(out=GQ[:, l, 0:1], in0=VECS[:, l, V_NAQ:V_NAQ + 1],
                                                       scalar1=0.125, scalar2=None, op0=ALU.mult),
                 r=["vecs"], w=["gq"])
            P.op("dve", lambda e, l=l: e.tensor_scalar(out=GQ[:, l, 1:2], in0=VECS[:, l, V_MQ:V_MQ + 1],
                                                       scalar1=96.0 ** -0.5, scalar2=None, op0=ALU.mult),
                 r=["vecs"], w=["gq"])
        P.barrier()

        def norm_phase(l, ni, b, lay_):
            SQ = [alloc(lay_, [512], BF16) for _ in range(3)]
            RS = [alloc(lay_, [512], F32) for _ in range(2)]
            TMP = [alloc(lay_, [512], F32) for _ in range(3)]
            cbsh = 0 if ni == 0 else 3
            sqi = 0
            tmi = 0
            for ci, (c0, n) in enumerate(CHUNKS):
                col = b if ci < 4 else 4
                pst, psk = PS()
                for f in range(8):
                    sq = SQ[sqi % 3]
                    sqk = ("nsq", sqi % 3)
                    sqi += 1
                    act(sq[:, 0:n], XT[:, f, c0:c0 + n], AF.Square, r=[("xt", f, ci)], w=[sqk])
                    mm(pst[:, 0:n], ONES, sq[:, 0:n], f == 0, f == 7, r=[sqk, "ones"], w=[psk])
                rs = RS[ci % 2]
                rsk = ("nrs", ci % 2)
                rstd_from_ps(pst[:, 0:n], 128, n, 1.0 / D, rs, rsk, psk)
                for f in range(8):
                    tm = TMP[tmi % 3]
                    tmk = ("ntm", tmi % 3)
                    tmi += 1
                    P.op("dve", lambda e, tm=tm, f=f, c0=c0, n=n, col=col, rs=rs: e.scalar_tensor_tensor(
                        out=tm[:, 0:n], in0=XT[:, f, c0:c0 + n], scalar=GS[:, l, ni, f, col:col + 1],
                        in1=rs[:, 0:n], op0=ALU.mult, op1=ALU.mult),
                        r=[("xt", f, ci), "gs", rsk], w=[tmk])
                    act(HT[:, f, c0:c0 + n], tm[:, 0:n], AF.Identity, r=[tmk, "modv"], w=[("ht", f, ci)],
                        bias=MODV[:, l, cbsh * 8 + f, col:col + 1])

        def htk(ci):
            return [("ht", f, ci) for f in range(8)]

        def merge_phase(l, b, bidx, OB, obk, need_ctx, lay_):
            WG = alloc(lay_, [8, 1024], BF16)
            WBO = alloc(lay_, [4, 1024], BF16)
            WO = alloc(lay_, [8, 1024], BF16)
            SG = [alloc(lay_, [512], BF16) for _ in range(2)]
            TM = [alloc(lay_, [8, 512], BF16) for _ in range(2)]
            kg, kb, ko = U("wg"), U("wbo"), U("wo")
            wload(WG, wsrc(w_in_d, l, 0, D, C_GT + bidx * 1024, 1024), kg)
            wload(WBO, wsrc(w_bo_d[bidx], l, 0, 512, 0, 1024), kb)
            wload(WO, wsrc(w_o_d, l, 0, D, 0, 1024), ko)
            sgi = 0
            for ci, (c0, n) in enumerate(CHUNKS):
                if ci == 4 and not need_ctx:
                    continue
                col = b if ci < 4 else 4
                tm = TM[ci % 2]
                for f in range(8):
                    psg, psgk = PS()
                    for kc in range(8):
                        mm(psg[:, 0:n], WG[:, kc, f * 128:(f + 1) * 128], HT[:, kc, c0:c0 + n], kc == 0, kc == 7,
                           r=[kg, ("ht", kc, ci)], w=[psgk])
                    psp, pspk = PS()
                    for kc in range(4):
                        mm(psp[:, 0:n], WBO[:, kc, f * 128:(f + 1) * 128], OB[:, kc, c0:c0 + n], kc == 0, kc == 3,
                           r=[kb, (obk, kc, ci)], w=[pspk])
                    sg = SG[sgi % 2]
                    sgk = ("msg", sgi % 2)
                    sgi += 1
                    act(sg[:, 0:n], psg[:, 0:n], AF.Sigmoid, r=[psgk], w=[sgk])
                    P.op("dve", lambda e, tm=tm, f=f, n=n, psp=psp, sg=sg: e.tensor_tensor(
                        out=tm[:, f, 0:n], in0=psp[:, 0:n], in1=sg[:, 0:n], op=ALU.mult),
                        r=[pspk, sgk], w=[("mtm", ci % 2, f)])
                for f2 in range(8):
                    psy, psyk = PS()
                    for f in range(8):
                        mm(psy[:, 0:n], WO[:, f, f2 * 128:(f2 + 1) * 128], tm[:, f, 0:n], f == 0, f == 7,
                           r=[ko, ("mtm", ci % 2, f)], w=[psyk])
                    P.op("dve", lambda e, f2=f2, c0=c0, n=n, psy=psy, col=col: e.scalar_tensor_tensor(
                        out=XT[:, f2, c0:c0 + n], in0=psy[:, 0:n], scalar=MODV[:, l, 2 * 8 + f2, col:col + 1],
                        in1=XT[:, f2, c0:c0 + n], op0=ALU.mult, op1=ALU.add),
                        r=[psyk, "modv", ("xt", f2, ci)], w=[("xt", f2, ci)])

        def na_phase(l, b, need_ctx, lay_):
            ONA = alloc(lay_, [4, NT], BF16)
            lay2 = Lay(lay_.p, lay_.limit)
            WQ = [alloc(lay2, [8, 128], BF16) for _ in range(2)]
            WK = [alloc(lay2, [8, 128], BF16) for _ in range(2)]
            WV = [alloc(lay2, [8, 128], BF16) for _ in range(2)]
            TB = [alloc(lay2, [2, 1024], BF16) for _ in range(2)]
            CM = alloc(lay2, [1024], BF16)
            QT = alloc(lay2, [NT], BF16)
            KT = alloc(lay2, [NT], BF16)
            VA = alloc(lay2, [NKT, 192], BF16)
            SQ = [alloc(lay2, [512], BF16) for _ in range(2)]
            RS = [alloc(lay2, [512], F32) for _ in range(2)]
            PT = [alloc(lay2, [512], BF16) for _ in range(3)]
            RC = [alloc(lay2, [64], F32) for _ in range(2)]
            wload(CM, cmask_d[:, :], "na_cm")
            P.op("pool", lambda e: e.memset(VA, 1.0), w=["na_va"])
            cnt = {"sq": 0, "rs": 0, "pt": 0, "rc": 0}
            for hp in range(4):
                s = hp % 2
                kq, kk, kv, ktb = ("na_wq", s), ("na_wk", s), ("na_wv", s), ("na_tb", s)
                wload(WQ[s], wsrc(w_in_d, l, 0, D, C_NQ + hp * 128, 128), kq)
                wload(WK[s], wsrc(w_in_d, l, 0, D, C_NK + hp * 128, 128), kk)
                wload(WV[s], wsrc(w_in_d, l, 0, D, C_NV + hp * 128, 128), kv)
                wload(TB[s], tb_d[l, 2 * hp:2 * hp + 2].rearrange("h p n -> p h n"), ktb)
                for e2 in range(2):
                    P.op("dve", lambda e, s=s, e2=e2: e.tensor_tensor(out=TB[s][:, e2, :], in0=TB[s][:, e2, :], in1=CM,
                                                                     op=ALU.add), r=[ktb, "na_cm"], w=[ktb])
                for ci, (c0, n) in enumerate(CHUNKS):
                    for which in range(2):
                        if which == 0 and ci == 4 and not need_ctx:
                            continue
                        Wt, wk_ = (WQ[s], kq) if which == 0 else (WK[s], kk)
                        dst = QT if which == 0 else KT
                        dk = ("na_q", ci) if which == 0 else ("na_k", ci)
                        gain = GQ[:, l, 0:1] if which == 0 else VECS[:, l, V_NAK:V_NAK + 1]
                        gk = "gq" if which == 0 else "vecs"
                        pst, psk = PS()
                        for kc in range(8):
                            mm(pst[:, 0:n], Wt[:, kc, :], HT[:, kc, c0:c0 + n], kc == 0, kc == 7,
                               r=[wk_, ("ht", kc, ci)], w=[psk])
                        sq = SQ[cnt["sq"] % 2]
                        sqk = ("na_sq", cnt["sq"] % 2)
                        cnt["sq"] += 1
                        act(sq[:, 0:n], pst[:, 0:n], AF.Square, r=[psk], w=[sqk])
                        ps2, ps2k = PS()
                        mm(ps2[:, 0:n], BD, sq[:, 0:n], True, True, r=["bd", sqk], w=[ps2k])
                        rs = RS[cnt["rs"] % 2]
                        rsk = ("na_rs", cnt["rs"] % 2)
                        cnt["rs"] += 1
                        rstd_from_ps(ps2[:, 0:n], 128, n, 1.0 / 64, rs, rsk, ps2k)
                        P.op("dve", lambda e, dst=dst, c0=c0, n=n, pst=pst, gain=gain, rs=rs: e.scalar_tensor_tensor(
                            out=dst[:, c0:c0 + n], in0=pst[:, 0:n], scalar=gain, in1=rs[:, 0:n],
                            op0=ALU.mult, op1=ALU.mult), r=[psk, gk, rsk], w=[dk])
                for g in range(5):
                    tts = list(range(g * 4, min(g * 4 + 4, NKT)))
                    pst, psk = PS()
                    for j, tt in enumerate(tts):
                        ci = min(tt // 4, 4)
                        for kc in range(8):
                            mm(pst[:, j * 128:(j + 1) * 128], HT[:, kc, tt * 128:(tt + 1) * 128], WV[s][:, kc, :],
                               kc == 0, kc == 7, r=[kv, ("ht", kc, ci)], w=[psk])
                    nt_ = len(tts)
                    src = pst[:, 0:nt_ * 128].rearrange("p (t e d) -> p t e d", t=nt_, e=2)
                    dstv = VA[:, tts[0]:tts[0] + nt_, :].rearrange("p t (e d) -> p t e d", e=3)[:, :, 0:3:2, :]
                    P.op("act", lambda e, dstv=dstv, src=src: e.copy(out=dstv, in_=src), r=[psk], w=["na_va"])
                for r_ in range(32):
                    rs0 = min(max(r_ - 4, 0), 24)
                    kr0 = 2 * (rs0 // 2)
                    nwin = 4 if rs0 % 2 == 0 else 5
                    ncols = (nwin + 2) * 64
                    qci = r_ // 8
                    for e2 in range(2):
                        pb = e2 * 64
                        pst, psk = PS()
                        for j in range(nwin):
                            tk = kr0 // 2 + j
                            dr_first = kr0 + 2 * j - r_ + 7
                            assert 0 <= dr_first <= 13, dr_first
                            mm(pst[:, j * 64:(j + 1) * 64], KT[pb:pb + 64, tk * 128:(tk + 1) * 128],
                               QT[pb:pb + 64, r_ * 64:(r_ + 1) * 64], True, False,
                               r=[("na_k", tk // 4), ("na_q", qci)], w=[psk])
                            mm(pst[:, j * 64:(j + 1) * 64], IDB, TB[s][:, e2, dr_first * 64:(dr_first + 1) * 64],
                               False, True, r=["idb", ktb], w=[psk])
                        for tc in range(2):
                            jj = nwin + tc
                            mm(pst[:, jj * 64:(jj + 1) * 64], KT[pb:pb + 64, T + tc * 128:T + (tc + 1) * 128],
                               QT[pb:pb + 64, r_ * 64:(r_ + 1) * 64], True, True,
                               r=[("na_k", 4), ("na_q", qci)], w=[psk])
                        pt = PT[cnt["pt"] % 3]
                        ptk = ("na_pt", cnt["pt"] % 3)
                        cnt["pt"] += 1
                        act(pt[:, 0:ncols], pst[:, 0:ncols], AF.Exp, r=[psk], w=[ptk])
                        if nwin == 5:
                            P.op("pool", lambda e, pt=pt: e.memset(pt[0:64, 0:64], 0.0), r=[], w=[ptk])
                            P.op("pool", lambda e, pt=pt: e.memset(pt[64:128, 256:320], 0.0), r=[], w=[ptk])
                        po, pok = PS()
                        for jj in range(nwin + 2):
                            tile = (kr0 // 2 + jj) if jj < nwin else (16 + jj - nwin)
                            mm(po[:, 0:64], VA[:, tile, e2 * 64:e2 * 64 + 128], pt[:, jj * 64:(jj + 1) * 64],
                               jj == 0, jj == nwin + 1, r=["na_va", ptk], w=[pok])
                        rc = RC[cnt["rc"] % 2]
                        rck = ("na_rc", cnt["rc"] % 2)
                        cnt["rc"] += 1
                        sb = 64 - pb
                        P.op("dve", lambda e, rc=rc, po=po, pb=pb, sb=sb: e.reciprocal(
                            out=rc[pb:pb + 64, :], in_=po[sb:sb + 64, 0:64]), r=[pok], w=[rck])
                        P.op("dve", lambda e, rc=rc, po=po, pb=pb, hp=hp, r_=r_: e.tensor_tensor(
                            out=ONA[pb:pb + 64, hp, r_ * 64:(r_ + 1) * 64], in0=po[pb:pb + 64, 0:64],
                            in1=rc[pb:pb + 64, :], op=ALU.mult), r=[pok, rck], w=[("ona", hp, qci)])
                if need_ctx:
                    for e2 in range(2):
                        pb = e2 * 64
                        pst, psk = PS()
                        for tc in range(2):
                            mm(pst[:, tc * 256:(tc + 1) * 256], KT[pb:pb + 64, T + tc * 128:T + (tc + 1) * 128],
                               QT[pb:pb + 64, T:NT], True, True, r=[("na_k", 4), ("na_q", 4)], w=[psk])
                        pt = PT[cnt["pt"] % 3]
                        ptk = ("na_pt", cnt["pt"] % 3)
                        cnt["pt"] += 1
                        act(pt[:, 0:512], pst[:, 0:512], AF.Exp, r=[psk], w=[ptk])
                        po, pok = PS()
                        for tc in range(2):
                            mm(po[:, 0:256], VA[:, 16 + tc, e2 * 64:e2 * 64 + 128], pt[:, tc * 256:(tc + 1) * 256],
                               tc == 0, tc == 1, r=["na_va", ptk], w=[pok])
                        rs = RS[cnt["rs"] % 2]
                        rsk = ("na_rs", cnt["rs"] % 2)
                        cnt["rs"] += 1
                        sb = 64 - pb
                        P.op("dve", lambda e, rs=rs, po=po, pb=pb, sb=sb: e.reciprocal(
                            out=rs[pb:pb + 64, 0:256], in_=po[sb:sb + 64, 0:256]), r=[pok], w=[rsk])
                        P.op("dve", lambda e, rs=rs, po=po, pb=pb, hp=hp: e.tensor_tensor(
                            out=ONA[pb:pb + 64, hp, T:NT], in0=po[pb:pb + 64, 0:256],
                            in1=rs[pb:pb + 64, 0:256], op=ALU.mult), r=[pok, rsk], w=[("ona", hp, 4)])
            P.barrier()
            merge_phase(l, b, 0, ONA, "ona", need_ctx, Lay(lay_.p, lay_.limit))
            P.barrier()

        def mla_phase(l, b, need_ctx, lay_):
            OM = alloc(lay_, [4, NT], BF16)
            lay2 = Lay(lay_.p, lay_.limit)
            QL = alloc(lay2, [3, NT], BF16)
            KVL = alloc(lay2, [2, NT], BF16)
            rope_off = lay2.take(2 * T * 2)
            ROPE = A.view(rope_off, [2, T], BF16)
            KR = A.view(rope_off, [NT], BF16)
            WQB = alloc(lay2, [3, 768], BF16)
            WKVB = alloc(lay2, [2, 1024], BF16)
            WKP = alloc(lay2, [8, 2, 96], BF16)
            kqv_off = lay2.take(3 * NT * 2)
            KH = A.view(kqv_off, [NT], BF16)
            QH = A.view(kqv_off + NT * 2, [NT], BF16)
            VH = A.view(kqv_off + 2 * NT * 2, [NKT, 128], BF16)
            WIN = A.view(kqv_off, [8, 672], BF16)
            lay3 = lay2
            SQ = [alloc(lay3, [512], BF16) for _ in range(2)]
            RS = [alloc(lay3, [512], F32) for _ in range(2)]
            T1 = [alloc(lay3, [512], F32) for _ in range(1)]
            T2 = [alloc(lay3, [512], F32) for _ in range(1)]
            PT = [alloc(lay3, [512], BF16) for _ in range(2)]
            cnt = {"sq": 0, "rs": 0, "t": 0, "pt": 0}
            NSQ = 2
            wload(WIN, wsrc(w_in_d, l, 0, D, C_MQ, 672), "m_win")
            wload(WQB, wsrc(w_qb_d, l, 0, 384, 0, 768), "m_wqb")
            wload(WKVB, wsrc(w_kvb_d, l, 0, 256, 0, 1024), "m_wkvb")
            wload(ROPE[64:96], rope_d[:, 64:96, :].rearrange("a p t -> p a t"), "m_rope")
            P.op("pool", lambda e: e.memset(WKP, 0.0), w=["m_wkp"])
            P.op("pool", lambda e: e.tensor_copy(
                out=WKP[:, :, :, 0:64],
                in_=WKVB.rearrange("p k (h c) -> p h k c", h=8)[:, :, :, 0:64]), r=["m_wkvb"], w=["m_wkp"])
            for ci, (c0, n) in enumerate(CHUNKS):
                for grp, (ntile, coff, vg, dst, dk, inv) in enumerate((
                        (3, 0, V_QA, QL, "m_ql", 1.0 / 384), (2, 384, V_KVA, KVL, "m_kvl", 1.0 / 256))):
                    if grp == 0 and ci == 4 and not need_ctx:
                        continue
                    pss = []
                    for t in range(ntile):
                        pst, psk = PS()
                        for kc in range(8):
                            mm(pst[:, 0:n], WIN[:, kc, coff + t * 128:coff + (t + 1) * 128], HT[:, kc, c0:c0 + n],
                               kc == 0, kc == 7, r=["m_win", ("ht", kc, ci)], w=[psk])
                        pss.append((pst, psk))
                    ps2, ps2k = PS()
                    for t in range(ntile):
                        sq = SQ[cnt["sq"] % NSQ]
                        sqk = ("m_sq", cnt["sq"] % NSQ)
                        cnt["sq"] += 1
                        act(sq[:, 0:n], pss[t][0][:, 0:n], AF.Square, r=[pss[t][1]], w=[sqk])
                        mm(ps2[:, 0:n], ONES, sq[:, 0:n], t == 0, t == ntile - 1, r=["ones", sqk], w=[ps2k])
                    rs = RS[cnt["rs"] % 2]
                    rsk = ("m_rs", cnt["rs"] % 2)
                    cnt["rs"] += 1
                    rstd_from_ps(ps2[:, 0:n], 128, n, inv, rs, rsk, ps2k)
                    for t in range(ntile):
                        P.op("dve", lambda e, dst=dst, t=t, c0=c0, n=n, pst=pss[t][0], vg=vg, rs=rs:
                             e.scalar_tensor_tensor(out=dst[:, t, c0:c0 + n], in0=pst[:, 0:n],
                                                    scalar=VECS[:, l, vg + t:vg + t + 1], in1=rs[:, 0:n],
                                                    op0=ALU.mult, op1=ALU.mult),
                             r=[pss[t][1], "vecs", rsk], w=[(dk, t, ci)])
                pst, psk = PS()
                for kc in range(8):
                    mm(pst[0:32, 0:n], WIN[:, kc, 640:672], HT[:, kc, c0:c0 + n], kc == 0, kc == 7,
                       r=["m_win", ("ht", kc, ci)], w=[psk])
                P.op("act", lambda e, c0=c0, n=n, pst=pst: e.copy(out=KR[0:32, c0:c0 + n], in_=pst[0:32, 0:n]),
                     r=[psk], w=[("m_kr", ci)])

            P.barrier()
            P.op("pool", lambda e: e.memset(VH, 1.0), w=["m_vh"])

            def head_norm_rope(pst, psk, dst, dk, ci, c0, n, gain, gk):
                sq = SQ[cnt["sq"] % NSQ]
                sqk = ("m_sq", cnt["sq"] % NSQ)
                cnt["sq"] += 1
                act(sq[0:96, 0:n], pst[0:96, 0:n], AF.Square, r=[psk], w=[sqk])
                ps2, ps2k = PS()
                mm(ps2[0:96, 0:n], ONES[0:96, 0:96], sq[0:96, 0:n], True, True, r=["ones", sqk], w=[ps2k])
                rs = RS[cnt["rs"] % 2]
                rsk = ("m_rs", cnt["rs"] % 2)
                cnt["rs"] += 1
                rstd_from_ps(ps2[0:96, 0:n], 96, n, 1.0 / 96, rs, rsk, ps2k)
                P.op("dve", lambda e: e.scalar_tensor_tensor(
                    out=dst[0:96, c0:c0 + n], in0=pst[0:96, 0:n], scalar=gain, in1=rs[0:96, 0:n],
                    op0=ALU.mult, op1=ALU.mult), r=[psk, gk, rsk], w=[(dk, ci)])
                if ci < 4:
                    ps3, ps3k = PS()
                    mm(ps3[0:96, 0:n], RMAT[0:96, 0:96], dst[0:96, c0:c0 + n], True, True,
                       r=["rmat", (dk, ci)], w=[ps3k])
                    t1 = T1[0]
                    t2 = T2[0]
                    t1k = ("m_t1", 0)
                    t2k = ("m_t2", 0)
                    cnt["t"] += 1
                    P.op("pool", lambda e: e.tensor_tensor(out=t1[64:96, 0:n], in0=dst[64:96, c0:c0 + n],
                                                           in1=ROPE[64:96, 0, c0:c0 + n], op=ALU.mult),
                         r=[(dk, ci), "m_rope"], w=[t1k])
                    P.op("dve", lambda e: e.tensor_tensor(out=t2[64:96, 0:n], in0=ps3[64:96, 0:n],
                                                          in1=ROPE[64:96, 1, c0:c0 + n], op=ALU.mult),
                         r=[ps3k, "m_rope"], w=[t2k])
                    P.op("dve", lambda e: e.tensor_tensor(out=dst[64:96, c0:c0 + n], in0=t1[64:96, 0:n],
                                                          in1=t2[64:96, 0:n], op=ALU.add),
                         r=[t1k, t2k], w=[(dk, ci)])

            for h in range(8):
                hp, e2 = h // 2, h % 2
                pb = e2 * 64
                for ci, (c0, n) in enumerate(CHUNKS):
                    pst, psk = PS()
                    for kc in range(2):
                        mm(pst[0:96, 0:n], WKP[:, h, kc, :], KVL[:, kc, c0:c0 + n], kc == 0, False,
                           r=["m_wkp", ("m_kvl", kc, ci)], w=[psk])
                    mm(pst[0:96, 0:n], SHIFT[0:32, :], KR[0:32, c0:c0 + n], False, True,
                       r=["shift", ("m_kr", ci)], w=[psk])
                    head_norm_rope(pst, psk, KH, "m_kh", ci, c0, n, VECS[0:96, l, V_MK:V_MK + 1], "vecs")
                for g in range(3):
                    tts = list(range(g * 8, min(g * 8 + 8, NKT)))
                    pst, psk = PS()
                    for j, tt in enumerate(tts):
                        ci = min(tt // 4, 4)
                        for kc in range(2):
                            mm(pst[:, j * 64:(j + 1) * 64], KVL[:, kc, tt * 128:(tt + 1) * 128],
                               WKVB[:, kc, h * 128 + 64:h * 128 + 128], kc == 0, kc == 1,
                               r=["m_wkvb", ("m_kvl", kc, ci)], w=[psk])
                    nt_ = len(tts)
                    P.op("act", lambda e, pst=pst, nt_=nt_, t0=tts[0], pb=pb: e.copy(
                        out=VH[:, t0:t0 + nt_, pb:pb + 64],
                        in_=pst[:, 0:nt_ * 64].rearrange("p (t d) -> p t d", t=nt_)), r=[psk], w=["m_vh"])
                P.op("pool", lambda e, pb=pb: e.memset(VH[:, :, 64 - pb:128 - pb], 1.0), r=[], w=["m_vh"])
                for ci, (c0, n) in enumerate(CHUNKS):
                    if ci == 4 and not need_ctx:
                        continue
                    pst, psk = PS()
                    for kc in range(3):
                        mm(pst[0:96, 0:n], WQB[:, kc, h * 96:(h + 1) * 96], QL[:, kc, c0:c0 + n], kc == 0, kc == 2,
                           r=["m_wqb", ("m_ql", kc, ci)], w=[psk])
                    head_norm_rope(pst, psk, QH, "m_qh", ci, c0, n, GQ[0:96, l, 1:2], "gq")
                for ci, (c0, n) in enumerate(CHUNKS):
                    if ci == 4 and not need_ctx:
                        continue
                    kts = list(range(NKT)) if ci < 4 else [16, 17]
                    po, pok = PSL()
                    for i, kt in enumerate(kts):
                        pst, psk = PS()
                        mm(pst[:, 0:n], KH[0:96, kt * 128:(kt + 1) * 128], QH[0:96, c0:c0 + n], True, True,
                           r=[("m_kh", min(kt // 4, 4)), ("m_qh", ci)], w=[psk])
                        pt = PT[cnt["pt"] % 2]
                        ptk = ("m_pt", cnt["pt"] % 2)
                        cnt["pt"] += 1
                        act(pt[:, 0:n], pst[:, 0:n], AF.Exp, r=[psk], w=[ptk])
                        mm(po[:, 0:n], VH[:, kt, :], pt[:, 0:n], i == 0, i == len(kts) - 1, r=["m_vh", ptk], w=[pok])
                    rs = RS[cnt["rs"] % 2]
                    rsk = ("m_rs", cnt["rs"] % 2)
                    cnt["rs"] += 1
                    sb = 64 - pb
                    P.op("dve", lambda e, rs=rs, po=po, n=n, pb=pb, sb=sb: e.reciprocal(
                        out=rs[pb:pb + 64, 0:n], in_=po[sb:sb + 64, 0:n]), r=[pok], w=[rsk])
                    P.op("dve", lambda e, rs=rs, po=po, n=n, pb=pb, hp=hp, c0=c0: e.tensor_tensor(
                        out=OM[pb:pb + 64, hp, c0:c0 + n], in0=po[pb:pb + 64, 0:n], in1=rs[pb:pb + 64, 0:n],
                        op=ALU.mult), r=[pok, rsk], w=[("om", hp, ci)])
            P.barrier()
            merge_phase(l, b, 1, OM, "om", need_ctx, Lay(lay_.p, lay_.limit))
            P.barrier()

        def lru_phase(l, b, need_ctx, lay_):
            OL = alloc(lay_, [4, NT], BF16)
            lay2 = Lay(lay_.p, lay_.limit)
            WX_ = [alloc(lay2, [8, 128], BF16) for _ in range(2)]
            WG_ = [alloc(lay2, [8, 128], BF16) for _ in range(2)]
            WLRU = alloc(lay2, [2, 2, 4, 128], BF16)
            lux_off = lay2.take((T + 4 + LC + 4) * 4)
            LUXP = A.view(lux_off, [T + 4], F32)
            LUXC = A.view(lux_off + (T + 4) * 4, [LC + 4], F32)
            H1 = A.view(lux_off, [NT], F32)
            UU = alloc(lay2, [NT], F32)
            UB = alloc(lay2, [NT], BF16)
            AA = alloc(lay2, [NT], F32)
            BB = alloc(lay2, [NT], F32)
            HS = alloc(lay2, [NT], F32)
            TR = [alloc(lay2, [512], F32) for _ in range(2)]
            TI = [alloc(lay2, [512], F32) for _ in range(1)]
            TS = [alloc(lay2, [512], F32) for _ in range(1)]
            P.op("pool", lambda e: e.memset(WLRU, 0.0), w=["wlru"])
            for ax, wd in enumerate((wa_d, wx_d)):
                for d in range(2):
                    for par in range(2):
                        P.op("pool", lambda e, ax=ax, d=d, par=par, wd=wd: e.dma_start(
                            out=WLRU[par * 64:(par + 1) * 64, ax, d, :, par * 64:(par + 1) * 64],
                            in_=wd[l, d, par:8:2].rearrange("n c d -> c n d")), w=["wlru"], dma=True)
            P.op("dve", lambda e: e.memset(LUXP, 0.0), w=["l_luxp"])
            P.op("dve", lambda e: e.memset(LUXC, 0.0), w=["l_luxc"])
            tc_ = [0]
            for j in range(4):
                s = j % 2
                kx, kg = ("l_wx", s), ("l_wg", s)
                wload(WX_[s], wsrc(w_in_d, l, 0, D, C_LX + j * 128, 128), kx)
                wload(WG_[s], wsrc(w_in_d, l, 0, D, C_LG + j * 128, 128), kg)
                if j > 0:
                    P.op("dve", lambda e: e.memset(LUXP[:, 0:2], 0.0), w=["l_luxp"])
                    P.op("dve", lambda e: e.memset(LUXP[:, T + 2:T + 4], 0.0), w=["l_luxp"])
                    P.op("dve", lambda e: e.memset(LUXC[:, 0:2], 0.0), w=["l_luxc"])
                    P.op("dve", lambda e: e.memset(LUXC[:, LC + 2:LC + 4], 0.0), w=["l_luxc"])
                for ci, (c0, n) in enumerate(CHUNKS):
                    pst, psk = PS()
                    for kc in range(8):
                        mm(pst[:, 0:n], WX_[s][:, kc, :], HT[:, kc, c0:c0 + n], kc == 0, kc == 7,
                           r=[kx, ("ht", kc, ci)], w=[psk])
                    if ci < 4:
                        P.op("act", lambda e, pst=pst, c0=c0, n=n: e.copy(out=LUXP[:, 2 + c0:2 + c0 + n],
                                                                          in_=pst[:, 0:n]), r=[psk], w=["l_luxp"])
                    else:
                        P.op("act", lambda e, pst=pst, n=n: e.copy(out=LUXC[:, 2:2 + n], in_=pst[:, 0:n]),
                             r=[psk], w=["l_luxc"])
                for (src, sk, o0, n) in ((LUXP, "l_luxp", 0, T), (LUXC, "l_luxc", T, LC)):
                    P.op("dve", lambda e, src=src, o0=o0, n=n, j=j: e.tensor_scalar(
                        out=UU[:, o0:o0 + n], in0=src[:, 0:n], scalar1=VECS[:, l, V_CW + j:V_CW + j + 1],
                        scalar2=VECS[:, l, V_CB + j:V_CB + j + 1], op0=ALU.mult, op1=ALU.add),
                        r=[sk, "vecs"], w=["l_u"])
                    for jj in range(1, 4):
                        P.op("dve", lambda e, src=src, o0=o0, n=n, j=j, jj=jj: e.scalar_tensor_tensor(
                            out=UU[:, o0:o0 + n], in0=src[:, jj:jj + n],
                            scalar=VECS[:, l, V_CW + jj * 4 + j:V_CW + jj * 4 + j + 1], in1=UU[:, o0:o0 + n],
                            op0=ALU.mult, op1=ALU.add), r=[sk, "vecs", "l_u"], w=["l_u"])
                P.op("pool", lambda e: e.tensor_copy(out=UB, in_=UU), r=["l_u"], w=["l_ub"])
                for d in range(2):
                    for ci, (c0, n) in enumerate(CHUNKS):
                        psr, psrk = PS()
                        mm(psr[:, 0:n], WLRU[:, 0, d, j, :], UB[:, c0:c0 + n], True, True, r=["wlru", "l_ub"], w=[psrk])
                        psi, psik = PS()
                        mm(psi[:, 0:n], WLRU[:, 1, d, j, :], UB[:, c0:c0 + n], True, True, r=["wlru", "l_ub"], w=[psik])
                        k_ = tc_[0] % 2
                        tc_[0] += 1
                        tr, ti, ts = TR[k_], TI[0], TS[0]
                        trk, tik, tsk = ("l_tr", k_), ("l_ti", 0), ("l_ts", 0)
                        act(tr[:, 0:n], psr[:, 0:n], AF.Sigmoid, r=[psrk, "vecs"], w=[trk],
                            bias=VECS[:, l, V_BA + d * 4 + j:V_BA + d * 4 + j + 1])
                        act(ti[:, 0:n], psi[:, 0:n], AF.Sigmoid, r=[psik, "vecs"], w=[tik],
                            bias=VECS[:, l, V_BX + d * 4 + j:V_BX + d * 4 + j + 1])
                        act(AA[:, c0:c0 + n], tr[:, 0:n], AF.Exp, r=[trk, "cdec"], w=[("l_a", ci)],
                            scale=CDEC[:, l, d * 4 + j:d * 4 + j + 1])
                        act(ts[:, 0:n], AA[:, c0:c0 + n], AF.Square, r=[("l_a", ci)], w=[tsk])
                        act(ts[:, 0:n], ts[:, 0:n], AF.Sqrt, r=[tsk, "cst"], w=[tsk], bias=CST[:, 1:2], scale=-1.0)
                        P.op("pool", lambda e, ti=ti, ts=ts, n=n: e.tensor_tensor(out=ti[:, 0:n], in0=ti[:, 0:n],
                                                                               in1=ts[:, 0:n], op=ALU.mult),
                             r=[tik, tsk], w=[tik])
                        P.op("dve", lambda e, ti=ti, c0=c0, n=n: e.tensor_tensor(out=BB[:, c0:c0 + n], in0=ti[:, 0:n],
                                                                                in1=UU[:, c0:c0 + n], op=ALU.mult),
                             r=[tik, "l_u"], w=[("l_b", ci)])
                    HD = HS if d == 0 else H1
                    hk = "l_hs" if d == 0 else "l_h1"
                    hkw = [hk] if d == 0 else [hk, "l_luxp", "l_luxc"]
                    allab = [("l_a", ci) for ci in range(5)] + [("l_b", ci) for ci in range(5)]
                    if d == 0:
                        P.op("dve", lambda e, HD=HD: e.tensor_tensor_scan(
                            out=HD[:, T:NT], data0=AA[:, T:NT], data1=BB[:, T:NT], initial=0.0,
                            op0=ALU.mult, op1=ALU.add), r=allab, w=hkw)
                        P.op("dve", lambda e, HD=HD: e.tensor_tensor_scan(
                            out=HD[:, 0:T], data0=AA[:, 0:T], data1=BB[:, 0:T], initial=HD[:, NT - 1:NT],
                            op0=ALU.mult, op1=ALU.add), r=allab + [hk], w=hkw)
                    else:
                        P.op("dve", lambda e, HD=HD: e.tensor_tensor_scan(
                            out=HD[:, T:NT][:, ::-1], data0=AA[:, T:NT][:, ::-1], data1=BB[:, T:NT][:, ::-1],
                            initial=0.0, op0=ALU.mult, op1=ALU.add), r=allab, w=hkw)
                        P.op("dve", lambda e, HD=HD: e.tensor_tensor_scan(
                            out=HD[:, 0:T][:, ::-1], data0=AA[:, 0:T][:, ::-1], data1=BB[:, 0:T][:, ::-1],
                            initial=HD[:, T:T + 1], op0=ALU.mult, op1=ALU.add), r=allab + [hk], w=hkw)
                        P.op("pool", lambda e: e.tensor_tensor(out=HS, in0=HS, in1=H1, op=ALU.add),
                             r=["l_hs", "l_h1", "l_luxp", "l_luxc"], w=["l_hs"])
                for ci, (c0, n) in enumerate(CHUNKS):
                    if ci == 4 and not need_ctx:
                        continue
                    pst, psk = PS()
                    for kc in range(8):
                        mm(pst[:, 0:n], WG_[s][:, kc, :], HT[:, kc, c0:c0 + n], kc == 0, kc == 7,
                           r=[kg, ("ht", kc, ci)], w=[psk])
                    k_ = tc_[0] % 2
                    tc_[0] += 1
                    tr, ti, ts = TR[k_], TI[0], TS[0]
                    trk, tik, tsk = ("l_tr", k_), ("l_ti", 0), ("l_ts", 0)
                    act(tr[:, 0:n], pst[:, 0:n], AF.Identity, r=[psk], w=[trk])
                    act(ti[:, 0:n], pst[:, 0:n], AF.Square, r=[psk], w=[tik])
                    P.op("dve", lambda e, ti=ti, n=n: e.tensor_scalar(out=ti[:, 0:n], in0=ti[:, 0:n], scalar1=0.044715,
                                                                      scalar2=1.0, op0=ALU.mult, op1=ALU.add),
                         r=[tik], w=[tik])
                    P.op("dve", lambda e, ti=ti, tr=tr, n=n: e.tensor_tensor(out=ti[:, 0:n], in0=ti[:, 0:n],
                                                                           in1=tr[:, 0:n], op=ALU.mult),
                         r=[tik, trk], w=[tik])
                    act(ts[:, 0:n], ti[:, 0:n], AF.Sigmoid, r=[tik], w=[tsk], scale=1.5957691216057308)
                    P.op("pool", lambda e, ts=ts, tr=tr, n=n: e.tensor_tensor(out=ts[:, 0:n], in0=ts[:, 0:n],
                                                                            in1=tr[:, 0:n], op=ALU.mult),
                         r=[tsk, trk], w=[tsk])
                    P.op("dve", lambda e, ts=ts, c0=c0, n=n, j=j: e.tensor_tensor(
                        out=OL[:, j, c0:c0 + n], in0=ts[:, 0:n], in1=HS[:, c0:c0 + n], op=ALU.mult),
                        r=[tsk, "l_hs"], w=[("ol", j, ci)])
            P.barrier()
            merge_phase(l, b, 2, OL, "ol", need_ctx, Lay(lay_.p, lay_.limit))
            P.barrier()

        def ffn_phase(l, b, need_ctx, lay_):
            W1 = [alloc(lay_, [8, 1024], BF16) for _ in range(2)]
            W2 = [alloc(lay_, [8, 1024], BF16) for _ in range(2)]
            A1 = [alloc(lay_, [8, 512], BF16) for _ in range(2)]
            RL = [alloc(lay_, [512], BF16) for _ in range(3)]
            rli = 0
            a1i = 0
            for J in range(4):
                s = J % 2
                k1, k2 = ("f_w1", s), ("f_w2", s)
                wload(W1[s], wsrc(w_ff1_d, l, 0, D, J * 1024, 1024), k1)
                wload(W2[s], wsrc(w_ff2_d, l, J * 1024, 1024, 0, 1024), k2)
                for ci, (c0, n) in enumerate(CHUNKS):
                    if ci == 4 and not need_ctx:
                        continue
                    col = b if ci < 4 else 4
                    a1 = A1[a1i % 2]
                    a1s = a1i % 2
                    a1i += 1
                    for jj in range(8):
                        pst, psk = PS()
                        for kc in range(8):
                            mm(pst[:, 0:n], W1[s][:, kc, jj * 128:(jj + 1) * 128], HT[:, kc, c0:c0 + n],
                               kc == 0, kc == 7, r=[k1, ("ht", kc, ci)], w=[psk])
                        rl = RL[rli % 3]
                        rlk = ("f_rl", rli % 3)
                        rli += 1
                        act(rl[:, 0:n], pst[:, 0:n], AF.Relu, r=[psk], w=[rlk])
                        P.op("pool", lambda e, a1=a1, jj=jj, n=n, rl=rl: e.tensor_tensor(
                            out=a1[:, jj, 0:n], in0=rl[:, 0:n], in1=rl[:, 0:n], op=ALU.mult),
                            r=[rlk], w=[("f_a1", a1s, jj)])
                    for f2 in range(8):
                        psy, psyk = PS()
                        for jj in range(8):
                            mm(psy[:, 0:n], W2[s][:, jj, f2 * 128:(f2 + 1) * 128], a1[:, jj, 0:n], jj == 0, jj == 7,
                               r=[k2, ("f_a1", a1s, jj)], w=[psyk])
                        P.op("dve", lambda e, f2=f2, c0=c0, n=n, psy=psy, col=col: e.scalar_tensor_tensor(
                            out=XT[:, f2, c0:c0 + n], in0=psy[:, 0:n], scalar=MODV[:, l, 5 * 8 + f2, col:col + 1],
                            in1=XT[:, f2, c0:c0 + n], op0=ALU.mult, op1=ALU.add),
                            r=[psyk, "modv", ("xt", f2, ci)], w=[("xt", f2, ci)])

        def load_seq(b, lay_):
            ST = [alloc(lay_, [4, D], F32) for _ in range(2)]
            for g in range(5):
                s = g % 2
                stk = ("ld_st", s)
                if g < 4:
                    src = x_d[b, g * 512:(g + 1) * 512, :].rearrange("(t p) d -> p t d", p=128)
                    nt_ = 4
                else:
                    src = ctx_d[b, :, :].rearrange("(t p) d -> p t d", p=128)
                    nt_ = 2
                P.op("sp", lambda e, s=s, src=src, nt_=nt_: e.dma_start(out=ST[s][:, 0:nt_, :], in_=src),
                     w=[stk], dma=True)
                for f in range(8):
                    pst, psk = PS()
                    for t in range(nt_):
                        P.op("pe", lambda e, pst=pst, t=t, s=s, f=f: e.transpose(
                            out=pst[:, t * 128:(t + 1) * 128], in_=ST[s][:, t, f * 128:(f + 1) * 128], identity=IDF),
                            r=[stk, "idf"], w=[psk])
                    n = nt_ * 128
                    c0 = g * 512
                    if f % 2 == 0:
                        P.op("dve", lambda e, pst=pst, f=f, c0=c0, n=n: e.tensor_copy(out=XT[:, f, c0:c0 + n],
                                                                                    in_=pst[:, 0:n]),
                             r=[psk], w=[("xt", f, g)])
                    else:
                        P.op("act", lambda e, pst=pst, f=f, c0=c0, n=n: e.copy(out=XT[:, f, c0:c0 + n], in_=pst[:, 0:n]),
                             r=[psk], w=[("xt", f, g)])

        def store_seq(b, lay_):
            ST = [alloc(lay_, [D], F32) for _ in range(3)]
            for tt in range(16):
                s = tt % 3
                stk = ("st_st", s)
                ci = tt // 4
                for half in range(2):
                    pst, psk = PS()
                    for fj in range(4):
                        f = half * 4 + fj
                        P.op("pe", lambda e, pst=pst, fj=fj, f=f, tt=tt: e.transpose(
                            out=pst[:, fj * 128:(fj + 1) * 128], in_=XT[:, f, tt * 128:(tt + 1) * 128], identity=IDF),
                            r=[("xt", f, ci), "idf"], w=[psk])
                    if half == 0:
                        P.op("dve", lambda e, pst=pst, s=s: e.tensor_copy(out=ST[s][:, 0:512], in_=pst[:, :]),
                             r=[psk], w=[stk])
                    else:
                        P.op("act", lambda e, pst=pst, s=s: e.copy(out=ST[s][:, 512:1024], in_=pst[:, :]),
                             r=[psk], w=[stk])
                P.op("sp", lambda e, s=s, tt=tt: e.dma_start(out=out_d[b, tt * 128:(tt + 1) * 128, :], in_=ST[s]),
                     r=[stk], dma=True)

        for b in range(nseq):
            load_seq(b, Lay(PH0, PHL))
            P.barrier()
            for l in range(L):
                need_ctx = l < depth_total - 1
                norm_phase(l, 0, b, Lay(PH0, PHL))
                P.barrier()
                na_phase(l, b, need_ctx, Lay(PH0, PHL))
                mla_phase(l, b, need_ctx, Lay(PH0, PHL))
                lru_phase(l, b, need_ctx, Lay(PH0, PHL))
                norm_phase(l, 1, b, Lay(PH0, PHL))
                P.barrier()
                ffn_phase(l, b, need_ctx, Lay(PH0, PHL))
                P.barrier()
            store_seq(b, Lay(PH0, PHL))
            P.barrier()
        P.emit(stack)
    return nc


def _const_tables():
    ident = np.eye(128, dtype=np.float32)
    rmat = np.zeros((128, 96), np.float32)
    for m in range(64, 96):
        i = m - 64
        if i % 16 < 8:
            rmat[m + 8, m] = -1.0
        else:
            rmat[m - 8, m] = 1.0
    shiftm = np.zeros((128, 96), np.float32)
    for i in range(32):
        shiftm[i, 64 + i] = 1.0
    kcol = np.arange(128) % 64
    qcol = np.arange(64)
    cs = np.clip(qcol - 8, 0, GRID_W - 16)
    valid = (kcol[:, None] >= cs[None, :]) & (kcol[:, None] < cs[None, :] + 16)
    cm = np.where(valid, 0.0, NEG).astype(np.float32)
    colmask = np.tile(cm[:, None, :], (1, 16, 1)).reshape(128, 1024)
    t = np.arange(T)
    row = (t // GRID_W).astype(np.float32)
    col = (t % GRID_W).astype(np.float32)
    nf = 8
    inv = (10000.0 ** (-np.arange(nf, dtype=np.float32) / nf)).astype(np.float32)
    ropecs = np.zeros((2, 128, T), np.float32)
    for i in range(32):
        pos = row if i < 16 else col
        ang = (pos * inv[i % 8]).astype(np.float32)
        ropecs[0, 64 + i] = np.cos(ang)
        ropecs[1, 64 + i] = np.sin(ang)
    return dict(ident=ident, rmat=rmat, shiftm=shiftm, colmask=colmask, ropecs=ropecs)


def _pack_vecs(inp, L):
    vecs = np.zeros((L, 128, NV), np.float32)
    for l in range(L):
        v = vecs[l]
        v[:, V_BMOD:V_BMOD + 48] = inp["b_mod"][l].reshape(48, 128).T
        v[:, V_GMIX:V_GMIX + 8] = inp["g_mix"][l].reshape(8, 128).T
        v[:, V_GMLP:V_GMLP + 8] = inp["g_mlp"][l].reshape(8, 128).T
        v[:, V_NAQ] = np.tile(inp["na_q_gain"][l], 2)
        v[:, V_NAK] = np.tile(inp["na_k_gain"][l], 2)
        v[:, V_QA:V_QA + 3] = inp["mla_qa_gain"][l].reshape(3, 128).T
        v[:, V_KVA:V_KVA + 2] = inp["mla_kva_gain"][l].reshape(2, 128).T
        v[0:96, V_MQ] = inp["mla_q_gain"][l]
        v[0:96, V_MK] = inp["mla_k_gain"][l]
        for jj in range(4):
            v[:, V_CW + jj * 4:V_CW + jj * 4 + 4] = inp["lru_conv_w"][l, jj].reshape(4, 128).T
        v[:, V_CB:V_CB + 4] = inp["lru_conv_b"][l].reshape(4, 128).T
        for d in range(2):
            v[:, V_BA + d * 4:V_BA + d * 4 + 4] = inp["lru_ba"][l, d].reshape(4, 128).T
            v[:, V_BX + d * 4:V_BX + d * 4 + 4] = inp["lru_bx"][l, d].reshape(4, 128).T
            v[:, V_LAM + d * 4:V_LAM + d * 4 + 4] = inp["lru_lambda"][l, d].reshape(4, 128).T
    return vecs


def _pack_tb(rpb, L):
    p = np.arange(128)
    i = np.arange(16)
    q = np.arange(64)
    dr = np.minimum(i[None, :] + (p[:, None] >= 64), 14)
    dc = np.clip((p % 64)[:, None] - q[None, :] + 15, 0, 30)
    tb = rpb[:L][:, :, dr[:, :, None], dc[:, None, :]]
    return np.ascontiguousarray(tb.reshape(L, 8, 128, 1024)).astype(np.float32)


def host_inputs(inp, core, nseq=BPC, L=DEPTH, shared=None):
    b0 = core * nseq
    m = {}
    m["x"] = np.ascontiguousarray(inp["x"][b0:b0 + nseq])
    m["ctx"] = np.ascontiguousarray(inp["ctx"][b0:b0 + nseq])
    cvecs = np.zeros((5, D), np.float32)
    cvecs[0:nseq] = inp["c"][b0:b0 + nseq]
    cvecs[4] = inp["c_ctx"]
    m["ct"] = np.ascontiguousarray(cvecs.reshape(5, 8, 128).transpose(2, 1, 0)).reshape(128, 40)
    m.update(shared)
    return m


def shared_inputs(inp, L=DEPTH):
    sh = dict(_const_tables())
    sh["vecs"] = _pack_vecs(inp, L)
    sh["tbsrc"] = _pack_tb(np.asarray(inp["na_rpb"]), L)
    for k in ("w_mod", "w_in", "w_q_b", "w_kv_b", "lru_wa", "lru_wx", "w_na_o", "w_mla_o", "w_lru_o", "w_o",
              "w_ff1", "w_ff2"):
        sh[k] = np.ascontiguousarray(np.asarray(inp[k])[:L])
    return sh


_NC_CACHE = {}
SEQ_PER_LAUNCH = 1


def kernel(**inputs):
    inp = {k: np.asarray(v) for k, v in inputs.items()}
    nsl = SEQ_PER_LAUNCH
    if nsl not in _NC_CACHE:
        _NC_CACHE[nsl] = build_program(nseq=nsl)
    nc = _NC_CACHE[nsl]
    sh = shared_inputs(inp)
    B = inp["x"].shape[0]
    out = np.empty((B, T, D), np.float32)
    nlaunch = BPC // nsl
    for k in range(nlaunch):
        in_maps = [host_inputs(inp, k * NCORES + c, nseq=nsl, shared=sh) for c in range(NCORES)]
        res = run_bass_kernel_spmd(nc, in_maps, core_ids=list(range(NCORES)))
        for c in range(NCORES):
            b0 = (k * NCORES + c) * nsl
            out[b0:b0 + nsl] = res.results[c]["out"]
    return out
```

```python
import contextlib
import numpy as np
import concourse.bass as bass
import concourse.mybir as mybir
from concourse.bass_utils import run_bass_kernel_spmd

F32 = mybir.dt.float32
BF16 = mybir.dt.bfloat16
AF = mybir.ActivationFunctionType
ALU = mybir.AluOpType

D = 1024
T = 2048
LC = 256
NT = T + LC
DEPTH = 4
NCORES = 8
BPC = 4
GRID_W = 64
NKT = NT // 128
IN_COLS = 6304
C_NQ, C_NK, C_NV, C_MQ, C_MKV, C_MR, C_LX, C_LG, C_GT = 0, 512, 1024, 1536, 1920, 2176, 2208, 2720, 3232
EPS = 1e-6
NEG = -30000.0
CHUNKS = [(0, 512), (512, 512), (1024, 512), (1536, 512), (2048, 256)]

V_BMOD, V_GMIX, V_GMLP, V_NAQ, V_NAK, V_QA, V_KVA, V_MQ, V_MK, V_CW, V_CB, V_BA, V_BX, V_LAM = (
    0, 48, 56, 64, 65, 66, 69, 71, 72, 73, 89, 93, 101, 109)
NV = 117

ENGS = ("pe", "act", "dve", "pool", "sp")
NDMA_SLOTS = 12


class Op:
    __slots__ = ("eng", "fn", "dma", "deps", "idx", "sig", "cnt", "slot", "val")

    def __init__(self, eng, fn, dma):
        self.eng = eng
        self.fn = fn
        self.dma = dma
        self.deps = set()
        self.sig = False
        self.cnt = 0
        self.slot = -1
        self.val = 0


class Prog:
    def __init__(self, nc, same_engine_sync=True):
        self.nc = nc
        self.ops = []
        self.per_eng = {e: [] for e in ENGS}
        self.res = {}
        self.same_engine_sync = same_engine_sync
        self.pending_barrier = {e: None for e in ENGS}
        self.live_dma = []

    def op(self, eng, fn, r=(), w=(), dma=False):
        o = Op(eng, fn, dma)
        o.idx = len(self.ops)
        deps = o.deps
        res = self.res
        for k in r:
            ent = res.get(k)
            if ent is None:
                ent = res[k] = [None, []]
            if ent[0] is not None:
                deps.add(ent[0])
            ent[1].append(o)
        for k in w:
            ent = res.get(k)
            if ent is None:
                ent = res[k] = [None, []]
            if ent[0] is not None:
                deps.add(ent[0])
            for rd in ent[1]:
                if rd is not o:
                    deps.add(rd)
            ent[0] = o
            ent[1] = []
        deps.discard(o)
        pb = self.pending_barrier[eng]
        if pb is not None:
            deps.update(pb)
            self.pending_barrier[eng] = None
        self.ops.append(o)
        self.per_eng[eng].append(o)
        if dma:
            self.live_dma.append(o)
        return o

    def barrier(self):
        last = []
        for e in ENGS:
            for o in reversed(self.per_eng[e]):
                if not o.dma:
                    last.append(o)
                    break
        last.extend(self.live_dma)
        self.live_dma = []
        for e in ENGS:
            cur = self.pending_barrier[e]
            s = set(last)
            if cur is not None:
                s |= cur
            self.pending_barrier[e] = s

    def emit(self, stack):
        nc = self.nc
        ses = self.same_engine_sync
        for o in self.ops:
            for d in o.deps:
                if d.dma:
                    continue
                if d.eng != o.eng:
                    d.sig = True
                elif o.dma:
                    d.sig = True
                elif ses and d.eng != "pe":
                    d.sig = True
        engsem = {e: stack.enter_context(nc.semaphore("s_" + e)) for e in ENGS}
        dmasem = {e: [stack.enter_context(nc.semaphore("d_%s%d" % (e, i))) for i in range(NDMA_SLOTS)]
                  for e in ("sp", "pool", "act")}
        for e in ENGS:
            c = 0
            for o in self.per_eng[e]:
                if not o.dma and o.sig:
                    c += 1
                    o.cnt = c
        slot_val = {e: [0] * NDMA_SLOTS for e in dmasem}
        slot_prev = {e: [None] * NDMA_SLOTS for e in dmasem}
        nd = {e: 0 for e in dmasem}
        for o in self.ops:
            if o.dma:
                e = o.eng
                s = nd[e] % NDMA_SLOTS
                nd[e] += 1
                prev = slot_prev[e][s]
                if prev is not None:
                    o.deps.add(prev)
                slot_val[e][s] += 16
                o.slot = s
                o.val = slot_val[e][s]
                slot_prev[e][s] = o

        def emit_eng(e, eo):
            seen = {}
            for o in self.per_eng[e]:
                waits = {}
                for d in o.deps:
                    if d.dma:
                        sem = dmasem[d.eng][d.slot]
                        v = d.val
                    else:
                        if not d.sig or (d.eng == "pe" and e == "pe"):
                            continue
                        sem = engsem[d.eng]
                        v = d.cnt
                    key = id(sem)
                    if seen.get(key, 0) >= v:
                        continue
                    if key not in waits or waits[key][1] < v:
                        waits[key] = (sem, v)
                wl = list(waits.items())
                emb = None
                if wl and not o.dma:
                    emb = wl.pop()
                for key, (sem, v) in wl:
                    eo.wait_ge(sem, v)
                    seen[key] = v
                ins = o.fn(eo)
                if emb is not None:
                    ins._wait_ge(emb[1][0], emb[1][1])
                    seen[emb[0]] = emb[1][1]
                if o.dma:
                    ins.then_inc(dmasem[e][o.slot], 16)
                elif o.sig:
                    ins.then_inc(engsem[e], 1)
            return seen

        block = stack.enter_context(nc.Block())

        @block.tensor
        def _(eo):
            emit_eng("pe", eo)

        @block.scalar
        def _(eo):
            emit_eng("act", eo)

        @block.vector
        def _(eo):
            emit_eng("dve", eo)

        @block.gpsimd
        def _(eo):
            emit_eng("pool", eo)

        @block.sync
        def _(eo):
            seen = emit_eng("sp", eo)
            for e2 in dmasem:
                for s in range(NDMA_SLOTS):
                    v = slot_val[e2][s]
                    if v > 0 and seen.get(id(dmasem[e2][s]), 0) < v:
                        eo.wait_ge(dmasem[e2][s], v)


class Arena:
    def __init__(self, nc, stack, nbytes):
        self.n = nbytes
        self.t = stack.enter_context(nc.sbuf_tensor("arena", [128, nbytes // 4], F32))

    def view(self, off, shape, dtype):
        esz = 2 if dtype == BF16 else 4
        n = 1
        for s in shape:
            n *= s
        nb = n * esz
        assert off % 4 == 0 and nb % 4 == 0, (off, nb)
        assert off + nb <= self.n, ("arena overflow", off, nb, self.n)
        ap = self.t[:, off // 4:(off + nb) // 4]
        if dtype != F32:
            ap = ap.bitcast(dtype)
        if len(shape) == 2:
            ap = ap.rearrange("p (a b) -> p a b", a=shape[0])
        elif len(shape) == 3:
            ap = ap.rearrange("p (a b c) -> p a b c", a=shape[0], b=shape[1])
        elif len(shape) == 4:
            ap = ap.rearrange("p (a b c d) -> p a b c d", a=shape[0], b=shape[1], c=shape[2])
        return ap


class Lay:
    def __init__(self, base, limit):
        self.p = base
        self.limit = limit

    def take(self, nbytes):
        nbytes = (nbytes + 31) // 32 * 32
        o = self.p
        self.p += nbytes
        assert self.p <= self.limit, ("layout overflow", self.p, self.limit)
        return o


def build_program(nseq=BPC, nlayers=DEPTH, depth_total=DEPTH, dbg=None):
    nc = bass.Bass("TRN2", target_bir_lowering=False)
    L = nlayers

    def din(name, shape):
        return nc.dram_tensor(name, list(shape), F32, kind="ExternalInput").ap()

    x_d = din("x", [nseq, T, D])
    ctx_d = din("ctx", [nseq, LC, D])
    ct_d = din("ct", [128, 8 * 5])
    vecs_d = din("vecs", [L, 128, NV])
    tb_d = din("tbsrc", [L, 8, 128, 1024])
    ident_d = din("ident", [128, 128])
    rmat_d = din("rmat", [128, 96])
    shift_d = din("shiftm", [128, 96])
    cmask_d = din("colmask", [128, 1024])
    rope_d = din("ropecs", [2, 128, T])
    w_mod_d = din("w_mod", [L, D, 6 * D])
    w_in_d = din("w_in", [L, D, IN_COLS])
    w_qb_d = din("w_q_b", [L, 384, 768])
    w_kvb_d = din("w_kv_b", [L, 256, 1024])
    wa_d = din("lru_wa", [L, 2, 8, 64, 64])
    wx_d = din("lru_wx", [L, 2, 8, 64, 64])
    w_bo_d = [din("w_na_o", [L, 512, D]), din("w_mla_o", [L, 512, D]), din("w_lru_o", [L, 512, D])]
    w_o_d = din("w_o", [L, D, D])
    w_ff1_d = din("w_ff1", [L, D, 4 * D])
    w_ff2_d = din("w_ff2", [L, 4 * D, D])
    out_d = nc.dram_tensor("out", [nseq, T, D], F32, kind="ExternalOutput").ap()
    dbg_d = None
    if dbg:
        dbg_d = nc.dram_tensor("dbg", [128, dbg["n"]], F32, kind="ExternalOutput").ap()

    stack = contextlib.ExitStack()
    with stack:
        P = Prog(nc)
        ARENA = 212480
        A = Arena(nc, stack, ARENA)
        lay = Lay(0, ARENA)

        def alloc(lay_, shape, dtype):
            n = 1
            for s in shape:
                n *= s
            return A.view(lay_.take(n * (2 if dtype == BF16 else 4)), shape, dtype)

        XT = alloc(lay, [8, NT], F32)
        HT = alloc(lay, [8, NT], BF16)
        IDF = alloc(lay, [128], F32)
        IDB = alloc(lay, [128], BF16)
        ONES = alloc(lay, [128], BF16)
        BD = alloc(lay, [128], BF16)
        RMAT = alloc(lay, [96], BF16)
        SHIFT = alloc(lay, [96], BF16)
        VECS = alloc(lay, [L, NV], F32)
        MODV = alloc(lay, [L, 48, 5], F32)
        GS = alloc(lay, [L, 2, 8, 5], F32)
        CDEC = alloc(lay, [L, 8], F32)
        GQ = alloc(lay, [L, 2], F32)
        CST = alloc(lay, [8], F32)
        PH0 = lay.p
        PHL = ARENA

        ps_tiles = [stack.enter_context(nc.psum_tensor("ps%d" % i, [128, 512], F32)) for i in range(8)]
        ps_ctr = [0]

        def PS():
            i = ps_ctr[0] % 6
            ps_ctr[0] += 1
            return ps_tiles[i], ("ps", i)

        psl_ctr = [0]

        def PSL():
            i = 6 + psl_ctr[0] % 2
            psl_ctr[0] += 1
            return ps_tiles[i], ("ps", i)

        uid = [0]

        def U(name):
            uid[0] += 1
            return (name, uid[0])

        def wsrc(wd, l, r0, nr, c0, ncol):
            return wd[l, r0:r0 + nr, c0:c0 + ncol].rearrange("(kc ki) n -> ki kc n", ki=128)

        def wload(dst, src, key):
            P.op("pool", lambda e: e.dma_start(out=dst, in_=src), w=[key], dma=True)

        def mm(ps_ap, lhsT, rhs, start, stop, r, w):
            P.op("pe", lambda e: e.matmul(ps_ap, lhsT=lhsT, rhs=rhs, start=start, stop=stop), r=r, w=w)

        def act(out, in_, func, r, w, bias=None, scale=None):
            kw = {}
            if bias is not None:
                kw["bias"] = bias
            if scale is not None:
                kw["scale"] = scale
            P.op("act", lambda e: e.activation(out=out, in_=in_, func=func, **kw), r=r, w=w)

        def rstd_from_ps(ps_ap, npart, n, inv_dim, tmp, tmpk, psk):
            act(tmp[0:npart, 0:n], ps_ap, AF.Sqrt, r=[psk, "cst"], w=[tmpk], bias=CST[0:npart, 0:1], scale=inv_dim)
            P.op("dve", lambda e: e.reciprocal(out=tmp[0:npart, 0:n], in_=tmp[0:npart, 0:n]), r=[tmpk], w=[tmpk])

        P.op("sp", lambda e: e.dma_start(out=IDF, in_=ident_d[:, :]), w=["idf"], dma=True)
        P.op("sp", lambda e: e.dma_start(out=VECS, in_=vecs_d.rearrange("l p n -> p l n")), w=["vecs"], dma=True)
        P.op("pool", lambda e: e.dma_start(out=IDB, in_=ident_d[:, :]), w=["idb"], dma=True)
        P.op("pool", lambda e: e.dma_start(out=RMAT, in_=rmat_d[:, :]), w=["rmat"], dma=True)
        P.op("pool", lambda e: e.dma_start(out=SHIFT, in_=shift_d[:, :]), w=["shift"], dma=True)
        P.op("dve", lambda e: e.memset(ONES, 1.0), w=["ones"])
        P.op("dve", lambda e: e.memset(BD, 0.0), w=["bd"])
        P.op("dve", lambda e: e.memset(BD[0:64, 0:64], 1.0), w=["bd"])
        P.op("dve", lambda e: e.memset(BD[64:128, 64:128], 1.0), w=["bd"])
        P.op("dve", lambda e: e.memset(CST[:, 0:1], EPS), w=["cst"])
        P.op("dve", lambda e: e.memset(CST[:, 1:2], 1.0), w=["cst"])
        P.op("dve", lambda e: e.memset(CST[:, 2:3], 0.0), w=["cst"])

        pl = Lay(PH0, PHL)
        CTF = alloc(pl, [8, 5], F32)
        CTB = alloc(pl, [8, 5], BF16)
        WM = [alloc(pl, [8, 1024], BF16) for _ in range(2)]
        P.op("sp", lambda e: e.dma_start(out=CTF, in_=ct_d.rearrange("p (a b) -> p a b", a=8)), w=["ctf"], dma=True)
        act(CTB, CTF, AF.Silu, r=["ctf"], w=["ctb"])
        wmi = 0
        for l in range(L):
            for cb in range(6):
                wb = WM[wmi % 2]
                wk = ("wm", wmi % 2)
                wmi += 1
                wload(wb, wsrc(w_mod_d, l, 0, D, cb * 1024, 1024), wk)
                pst, psk = PS()
                for f in range(8):
                    for kc in range(8):
                        mm(pst[:, f * 5:(f + 1) * 5], wb[:, kc, f * 128:(f + 1) * 128], CTB[:, kc, :],
                           kc == 0, kc == 7, r=[wk, "ctb"], w=[psk])
                for f in range(8):
                    t = cb * 8 + f
                    act(MODV[:, l, t, :], pst[:, f * 5:(f + 1) * 5], AF.Identity, r=[psk, "vecs"], w=["modv"],
                        bias=VECS[:, l, V_BMOD + t:V_BMOD + t + 1])
            for ni, (cbsc, vg) in enumerate(((1, V_GMIX), (4, V_GMLP))):
                for f in range(8):
                    P.op("dve", lambda e, l=l, ni=ni, f=f, cbsc=cbsc, vg=vg: e.tensor_scalar(
                        out=GS[:, l, ni, f, :], in0=MODV[:, l, cbsc * 8 + f, :], scalar1=1.0,
                        scalar2=VECS[:, l, vg + f:vg + f + 1], op0=ALU.add, op1=ALU.mult),
                        r=["modv", "vecs"], w=["gs"])
            act(CDEC[:, l, :], VECS[:, l, V_LAM:V_LAM + 8], AF.Exp, r=["vecs"], w=["cdec"], scale=-1.0)
            act(CDEC[:, l, :], CDEC[:, l, :], AF.Ln, r=["cdec", "cst"], w=["cdec"], bias=CST[:, 1:2])
            P.op("dve", lambda e, l=l: e.tensor_scalar(out=CDEC[:, l, :], in0=CDEC[:, l, :], scalar1=-8.0,
                                                       scalar2=None, op0=ALU.mult), r=["cdec"], w=["cdec"])
            P.op("dve", lambda e, l=l: e.tensor_scalar(out=GQ[:, l, 0:1], in0=VECS[:, l, V_NAQ:V_NAQ + 1],
                                                       scalar1=0.125, scalar2=None, op0=ALU.mult),
                 r=["vecs"], w=["gq"])
            P.op("dve", lambda e, l=l: e.tensor_scalar(out=GQ[:, l, 1:2], in0=VECS[:, l, V_MQ:V_MQ + 1],
                                                       scalar1=96.0 ** -0.5, scalar2=None, op0=ALU.mult),
                 r=["vecs"], w=["gq"])
        P.barrier()

        def norm_phase(l, ni, b, lay_):
            SQ = [alloc(lay_, [512], BF16) for _ in range(3)]
            RS = [alloc(lay_, [512], F32) for _ in range(2)]
            TMP = [alloc(lay_, [512], F32) for _ in range(3)]
            cbsh = 0 if ni == 0 else 3
            sqi = 0
            tmi = 0
            for ci, (c0, n) in enumerate(CHUNKS):
                col = b if ci < 4 else 4
                pst, psk = PS()
                for f in range(8):
                    sq = SQ[sqi % 3]
                    sqk = ("nsq", sqi % 3)
                    sqi += 1
                    act(sq[:, 0:n], XT[:, f, c0:c0 + n], AF.Square, r=[("xt", f, ci)], w=[sqk])
                    mm(pst[:, 0:n], ONES, sq[:, 0:n], f == 0, f == 7, r=[sqk, "ones"], w=[psk])
                rs = RS[ci % 2]
                rsk = ("nrs", ci % 2)
                rstd_from_ps(pst[:, 0:n], 128, n, 1.0 / D, rs, rsk, psk)
                for f in range(8):
                    tm = TMP[tmi % 3]
                    tmk = ("ntm", tmi % 3)
                    tmi += 1
                    P.op("dve", lambda e, tm=tm, f=f, c0=c0, n=n, col=col, rs=rs: e.scalar_tensor_tensor(
                        out=tm[:, 0:n], in0=XT[:, f, c0:c0 + n], scalar=GS[:, l, ni, f, col:col + 1],
                        in1=rs[:, 0:n], op0=ALU.mult, op1=ALU.mult),
                        r=[("xt", f, ci), "gs", rsk], w=[tmk])
                    act(HT[:, f, c0:c0 + n], tm[:, 0:n], AF.Identity, r=[tmk, "modv"], w=[("ht", f, ci)],
                        bias=MODV[:, l, cbsh * 8 + f, col:col + 1])

        def htk(ci):
            return [("ht", f, ci) for f in range(8)]

        def merge_phase(l, b, bidx, OB, obk, need_ctx, lay_):
            WG = alloc(lay_, [8, 1024], BF16)
            WBO = alloc(lay_, [4, 1024], BF16)
            WO = alloc(lay_, [8, 1024], BF16)
            SG = [alloc(lay_, [512], BF16) for _ in range(2)]
            TM = [alloc(lay_, [8, 512], BF16) for _ in range(2)]
            kg, kb, ko = U("wg"), U("wbo"), U("wo")
            wload(WG, wsrc(w_in_d, l, 0, D, C_GT + bidx * 1024, 1024), kg)
            wload(WBO, wsrc(w_bo_d[bidx], l, 0, 512, 0, 1024), kb)
            wload(WO, wsrc(w_o_d, l, 0, D, 0, 1024), ko)
            sgi = 0
            for ci, (c0, n) in enumerate(CHUNKS):
                if ci == 4 and not need_ctx:
                    continue
                col = b if ci < 4 else 4
                tm = TM[ci % 2]
                for f in range(8):
                    psg, psgk = PS()
                    for kc in range(8):
                        mm(psg[:, 0:n], WG[:, kc, f * 128:(f + 1) * 128], HT[:, kc, c0:c0 + n], kc == 0, kc == 7,
                           r=[kg, ("ht", kc, ci)], w=[psgk])
                    psp, pspk = PS()
                    for kc in range(4):
                        mm(psp[:, 0:n], WBO[:, kc, f * 128:(f + 1) * 128], OB[:, kc, c0:c0 + n], kc == 0, kc == 3,
                           r=[kb, (obk, kc, ci)], w=[pspk])
                    sg = SG[sgi % 2]
                    sgk = ("msg", sgi % 2)
                    sgi += 1
                    act(sg[:, 0:n], psg[:, 0:n], AF.Sigmoid, r=[psgk], w=[sgk])
                    P.op("dve", lambda e, tm=tm, f=f, n=n, psp=psp, sg=sg: e.tensor_tensor(
                        out=tm[:, f, 0:n], in0=psp[:, 0:n], in1=sg[:, 0:n], op=ALU.mult),
                        r=[pspk, sgk], w=[("mtm", ci % 2, f)])
                for f2 in range(8):
                    psy, psyk = PS()
                    for f in range(8):
                        mm(psy[:, 0:n], WO[:, f, f2 * 128:(f2 + 1) * 128], tm[:, f, 0:n], f == 0, f == 7,
                           r=[ko, ("mtm", ci % 2, f)], w=[psyk])
                    P.op("dve", lambda e, f2=f2, c0=c0, n=n, psy=psy, col=col: e.scalar_tensor_tensor(
                        out=XT[:, f2, c0:c0 + n], in0=psy[:, 0:n], scalar=MODV[:, l, 2 * 8 + f2, col:col + 1],
                        in1=XT[:, f2, c0:c0 + n], op0=ALU.mult, op1=ALU.add),
                        r=[psyk, "modv", ("xt", f2, ci)], w=[("xt", f2, ci)])

        def na_phase(l, b, need_ctx, lay_):
            ONA = alloc(lay_, [4, NT], BF16)
            lay2 = Lay(lay_.p, lay_.limit)
            WQ = [alloc(lay2, [8, 128], BF16) for _ in range(2)]
            WK = [alloc(lay2, [8, 128], BF16) for _ in range(2)]
            WV = [alloc(lay2, [8, 128], BF16) for _ in range(2)]
            TB = [alloc(lay2, [2, 1024], BF16) for _ in range(2)]
            CM = alloc(lay2, [1024], BF16)
            QT = alloc(lay2, [NT], BF16)
            KT = alloc(lay2, [NT], BF16)
            VA = alloc(lay2, [NKT, 192], BF16)
            SQ = [alloc(lay2, [512], BF16) for _ in range(2)]
            RS = [alloc(lay2, [512], F32) for _ in range(2)]
            PT = [alloc(lay2, [896], BF16) for _ in range(3)]
            RC = [alloc(lay2, [128], F32) for _ in range(2)]
            wload(CM, cmask_d[:, :], "na_cm")
            P.op("pool", lambda e: e.memset(VA, 1.0), w=["na_va"])
            cnt = {"sq": 0, "rs": 0, "pt": 0, "rc": 0}
            for hp in range(4):
                s = hp % 2
                kq, kk, kv, ktb = ("na_wq", s), ("na_wk", s), ("na_wv", s), ("na_tb", s)
                wload(WQ[s], wsrc(w_in_d, l, 0, D, C_NQ + hp * 128, 128), kq)
                wload(WK[s], wsrc(w_in_d, l, 0, D, C_NK + hp * 128, 128), kk)
                wload(WV[s], wsrc(w_in_d, l, 0, D, C_NV + hp * 128, 128), kv)
                wload(TB[s], tb_d[l, 2 * hp:2 * hp + 2].rearrange("h p n -> p h n"), ktb)
                for e2 in range(2):
                    P.op("dve", lambda e, s=s, e2=e2: e.tensor_tensor(out=TB[s][:, e2, :], in0=TB[s][:, e2, :], in1=CM,
                                                                     op=ALU.add), r=[ktb, "na_cm"], w=[ktb])
                for ci, (c0, n) in enumerate(CHUNKS):
                    for which in range(2):
                        if which == 0 and ci == 4 and not need_ctx:
                            continue
                        Wt, wk_ = (WQ[s], kq) if which == 0 else (WK[s], kk)
                        dst = QT if which == 0 else KT
                        dk = ("na_q", ci) if which == 0 else ("na_k", ci)
                        gain = GQ[:, l, 0:1] if which == 0 else VECS[:, l, V_NAK:V_NAK + 1]
                        gk = "gq" if which == 0 else "vecs"
                        pst, psk = PS()
                        for kc in range(8):
                            mm(pst[:, 0:n], Wt[:, kc, :], HT[:, kc, c0:c0 + n], kc == 0, kc == 7,
                               r=[wk_, ("ht", kc, ci)], w=[psk])
                        sq = SQ[cnt["sq"] % 2]
                        sqk = ("na_sq", cnt["sq"] % 2)
                        cnt["sq"] += 1
                        act(sq[:, 0:n], pst[:, 0:n], AF.Square, r=[psk], w=[sqk])
                        ps2, ps2k = PS()
                        mm(ps2[:, 0:n], BD, sq[:, 0:n], True, True, r=["bd", sqk], w=[ps2k])
                        rs = RS[cnt["rs"] % 2]
                        rsk = ("na_rs", cnt["rs"] % 2)
                        cnt["rs"] += 1
                        rstd_from_ps(ps2[:, 0:n], 128, n, 1.0 / 64, rs, rsk, ps2k)
                        P.op("dve", lambda e, dst=dst, c0=c0, n=n, pst=pst, gain=gain, rs=rs: e.scalar_tensor_tensor(
                            out=dst[:, c0:c0 + n], in0=pst[:, 0:n], scalar=gain, in1=rs[:, 0:n],
                            op0=ALU.mult, op1=ALU.mult), r=[psk, gk, rsk], w=[dk])
                for g in range(5):
                    tts = list(range(g * 4, min(g * 4 + 4, NKT)))
                    pst, psk = PS()
                    for j, tt in enumerate(tts):
                        ci = min(tt // 4, 4)
                        for kc in range(8):
                            mm(pst[:, j * 128:(j + 1) * 128], HT[:, kc, tt * 128:(tt + 1) * 128], WV[s][:, kc, :],
                               kc == 0, kc == 7, r=[kv, ("ht", kc, ci)], w=[psk])
                    nt_ = len(tts)
                    src = pst[:, 0:nt_ * 128].rearrange("p (t e d) -> p t e d", t=nt_, e=2)
                    dstv = VA[:, tts[0]:tts[0] + nt_, :].rearrange("p t (e d) -> p t e d", e=3)[:, :, 0:3:2, :]
                    P.op("act", lambda e, dstv=dstv, src=src: e.copy(out=dstv, in_=src), r=[psk], w=["na_va"])
                for i in range(16):
                    rows = (2 * i, 2 * i + 1)
                    rsl = [min(max(r_ - 4, 0), 24) for r_ in rows]
                    tiles = sorted(set(kr // 2 for rs0 in rsl for kr in range(rs0, rs0 + 8)))
                    nwin = len(tiles)
                    nblk = nwin + 2
                    qci = i // 4
                    for e2 in range(2):
                        pb = e2 * 64
                        psA, psAk = PS()
                        psB, psBk = PS()

                        def blk(j):
                            return (psA, psAk, (j % 4) * 128) if j < 4 else (psB, psBk, (j % 4) * 128)

                        for j, tk in enumerate(tiles):
                            pst, psk, co = blk(j)
                            ia = 2 * tk - rows[0] + 7
                            ipa = min(max(13 - ia, 0), 14)
                            mm(pst[:, co:co + 128], KT[pb:pb + 64, tk * 128:(tk + 1) * 128],
                               QT[pb:pb + 64, i * 128:(i + 1) * 128], True, False,
                               r=[("na_k", tk // 4), ("na_q", qci)], w=[psk])
                            mm(pst[:, co:co + 128], IDB, TB[s][:, e2, ipa * 64:(ipa + 2) * 64],
                               False, True, r=["idb", ktb], w=[psk])
                        for tc in range(2):
                            pst, psk, co = blk(nwin + tc)
                            mm(pst[:, co:co + 128], KT[pb:pb + 64, T + tc * 128:T + (tc + 1) * 128],
                               QT[pb:pb + 64, i * 128:(i + 1) * 128], True, True,
                               r=[("na_k", 4), ("na_q", qci)], w=[psk])
                        pt = PT[cnt["pt"] % 3]
                        ptk = ("na_pt", cnt["pt"] % 3)
                        cnt["pt"] += 1
                        nA = min(nblk, 4) * 128
                        nB = (nblk - 4) * 128
                        act(pt[:, 0:nA], psA[:, 0:nA], AF.Exp, r=[psAk], w=[ptk])
                        act(pt[:, 512:512 + nB], psB[:, 0:nB], AF.Exp, r=[psBk], w=[ptk])
                        for j, tk in enumerate(tiles):
                            for qi in range(2):
                                v0 = rsl[qi] <= 2 * tk < rsl[qi] + 8
                                v1 = rsl[qi] <= 2 * tk + 1 < rsl[qi] + 8
                                cc = j * 128 + qi * 64
                                if v0 and v1:
                                    continue
                                p0, p1 = (0, 128) if (not v0 and not v1) else ((0, 64) if not v0 else (64, 128))
                                P.op("pool", lambda e, pt=pt, p0=p0, p1=p1, cc=cc: e.memset(pt[p0:p1, cc:cc + 64], 0.0),
                                     r=[], w=[ptk])
                        po, pok = PS()
                        for j in range(nblk):
                            tile = tiles[j] if j < nwin else (16 + j - nwin)
                            mm(po[:, 0:128], VA[:, tile, e2 * 64:e2 * 64 + 128], pt[:, j * 128:(j + 1) * 128],
                               j == 0, j == nblk - 1, r=["na_va", ptk], w=[pok])
                        rc = RC[cnt["rc"] % 2]
                        rck = ("na_rc", cnt["rc"] % 2)
                        cnt["rc"] += 1
                        sb = 64 - pb
                        P.op("dve", lambda e, rc=rc, po=po, pb=pb, sb=sb: e.reciprocal(
                            out=rc[pb:pb + 64, :], in_=po[sb:sb + 64, 0:128]), r=[pok], w=[rck])
                        P.op("dve", lambda e, rc=rc, po=po, pb=pb, hp=hp, i=i: e.tensor_tensor(
                            out=ONA[pb:pb + 64, hp, i * 128:(i + 1) * 128], in0=po[pb:pb + 64, 0:128],
                            in1=rc[pb:pb + 64, :], op=ALU.mult), r=[pok, rck], w=[("ona", hp, qci)])
                if need_ctx:
                    for e2 in range(2):
                        pb = e2 * 64
                        pst, psk = PS()
                        for tc in range(2):
                            mm(pst[:, tc * 256:(tc + 1) * 256], KT[pb:pb + 64, T + tc * 128:T + (tc + 1) * 128],
                               QT[pb:pb + 64, T:NT], True, True, r=[("na_k", 4), ("na_q", 4)], w=[psk])
                        pt = PT[cnt["pt"] % 3]
                        ptk = ("na_pt", cnt["pt"] % 3)
                        cnt["pt"] += 1
                        act(pt[:, 0:512], pst[:, 0:512], AF.Exp, r=[psk], w=[ptk])
                        po, pok = PS()
                        for tc in range(2):
                            mm(po[:, 0:256], VA[:, 16 + tc, e2 * 64:e2 * 64 + 128], pt[:, tc * 256:(tc + 1) * 256],
                               tc == 0, tc == 1, r=["na_va", ptk], w=[pok])
                        rs = RS[cnt["rs"] % 2]
                        rsk = ("na_rs", cnt["rs"] % 2)
                        cnt["rs"] += 1
                        sb = 64 - pb
                        P.op("dve", lambda e, rs=rs, po=po, pb=pb, sb=sb: e.reciprocal(
                            out=rs[pb:pb + 64, 0:256], in_=po[sb:sb + 64, 0:256]), r=[pok], w=[rsk])
                        P.op("dve", lambda e, rs=rs, po=po, pb=pb, hp=hp: e.tensor_tensor(
                            out=ONA[pb:pb + 64, hp, T:NT], in0=po[pb:pb + 64, 0:256],
                            in1=rs[pb:pb + 64, 0:256], op=ALU.mult), r=[pok, rsk], w=[("ona", hp, 4)])
            P.barrier()
            merge_phase(l, b, 0, ONA, "ona", need_ctx, Lay(lay_.p, lay_.limit))
            P.barrier()

        def mla_phase(l, b, need_ctx, lay_):
            OM = alloc(lay_, [4, NT], BF16)
            lay2 = Lay(lay_.p, lay_.limit)
            QL = alloc(lay2, [3, NT], BF16)
            KVL = alloc(lay2, [2, NT], BF16)
            rope_off = lay2.take(2 * T * 2)
            ROPE = A.view(rope_off, [2, T], BF16)
            KR = A.view(rope_off, [NT], BF16)
            WQB = alloc(lay2, [3, 768], BF16)
            WKVB = alloc(lay2, [2, 1024], BF16)
            WKP = alloc(lay2, [8, 2, 96], BF16)
            kqv_off = lay2.take(3 * NT * 2)
            KH = A.view(kqv_off, [NT], BF16)
            QH = A.view(kqv_off + NT * 2, [NT], BF16)
            VH = A.view(kqv_off + 2 * NT * 2, [NKT, 128], BF16)
            WIN = A.view(kqv_off, [8, 672], BF16)
            lay3 = lay2
            SQ = [alloc(lay3, [512], BF16) for _ in range(2)]
            RS = [alloc(lay3, [512], F32) for _ in range(2)]
            T1 = [alloc(lay3, [512], F32) for _ in range(1)]
            T2 = [alloc(lay3, [512], F32) for _ in range(1)]
            PT = [alloc(lay3, [512], BF16) for _ in range(2)]
            cnt = {"sq": 0, "rs": 0, "t": 0, "pt": 0}
            NSQ = 2
            wload(WIN, wsrc(w_in_d, l, 0, D, C_MQ, 672), "m_win")
            wload(WQB, wsrc(w_qb_d, l, 0, 384, 0, 768), "m_wqb")
            wload(WKVB, wsrc(w_kvb_d, l, 0, 256, 0, 1024), "m_wkvb")
            wload(ROPE[64:96], rope_d[:, 64:96, :].rearrange("a p t -> p a t"), "m_rope")
            P.op("pool", lambda e: e.memset(WKP, 0.0), w=["m_wkp"])
            P.op("pool", lambda e: e.tensor_copy(
                out=WKP[:, :, :, 0:64],
                in_=WKVB.rearrange("p k (h c) -> p h k c", h=8)[:, :, :, 0:64]), r=["m_wkvb"], w=["m_wkp"])
            for ci, (c0, n) in enumerate(CHUNKS):
                for grp, (ntile, coff, vg, dst, dk, inv) in enumerate((
                        (3, 0, V_QA, QL, "m_ql", 1.0 / 384), (2, 384, V_KVA, KVL, "m_kvl", 1.0 / 256))):
                    if grp == 0 and ci == 4 and not need_ctx:
                        continue
                    pss = []
                    for t in range(ntile):
                        pst, psk = PS()
                        for kc in range(8):
                            mm(pst[:, 0:n], WIN[:, kc, coff + t * 128:coff + (t + 1) * 128], HT[:, kc, c0:c0 + n],
                               kc == 0, kc == 7, r=["m_win", ("ht", kc, ci)], w=[psk])
                        pss.append((pst, psk))
                    ps2, ps2k = PS()
                    for t in range(ntile):
                        sq = SQ[cnt["sq"] % NSQ]
                        sqk = ("m_sq", cnt["sq"] % NSQ)
                        cnt["sq"] += 1
                        act(sq[:, 0:n], pss[t][0][:, 0:n], AF.Square, r=[pss[t][1]], w=[sqk])
                        mm(ps2[:, 0:n], ONES, sq[:, 0:n], t == 0, t == ntile - 1, r=["ones", sqk], w=[ps2k])
                    rs = RS[cnt["rs"] % 2]
                    rsk = ("m_rs", cnt["rs"] % 2)
                    cnt["rs"] += 1
                    rstd_from_ps(ps2[:, 0:n], 128, n, inv, rs, rsk, ps2k)
                    for t in range(ntile):
                        P.op("dve", lambda e, dst=dst, t=t, c0=c0, n=n, pst=pss[t][0], vg=vg, rs=rs:
                             e.scalar_tensor_tensor(out=dst[:, t, c0:c0 + n], in0=pst[:, 0:n],
                                                    scalar=VECS[:, l, vg + t:vg + t + 1], in1=rs[:, 0:n],
                                                    op0=ALU.mult, op1=ALU.mult),
                             r=[pss[t][1], "vecs", rsk], w=[(dk, t, ci)])
                pst, psk = PS()
                for kc in range(8):
                    mm(pst[0:32, 0:n], WIN[:, kc, 640:672], HT[:, kc, c0:c0 + n], kc == 0, kc == 7,
                       r=["m_win", ("ht", kc, ci)], w=[psk])
                P.op("act", lambda e, c0=c0, n=n, pst=pst: e.copy(out=KR[0:32, c0:c0 + n], in_=pst[0:32, 0:n]),
                     r=[psk], w=[("m_kr", ci)])

            P.barrier()
            P.op("pool", lambda e: e.memset(VH, 1.0), w=["m_vh"])

            def head_norm_rope(pst, psk, dst, dk, ci, c0, n, gain, gk):
                sq = SQ[cnt["sq"] % NSQ]
                sqk = ("m_sq", cnt["sq"] % NSQ)
                cnt["sq"] += 1
                act(sq[0:96, 0:n], pst[0:96, 0:n], AF.Square, r=[psk], w=[sqk])
                ps2, ps2k = PS()
                mm(ps2[0:96, 0:n], ONES[0:96, 0:96], sq[0:96, 0:n], True, True, r=["ones", sqk], w=[ps2k])
                rs = RS[cnt["rs"] % 2]
                rsk = ("m_rs", cnt["rs"] % 2)
                cnt["rs"] += 1
                rstd_from_ps(ps2[0:96, 0:n], 96, n, 1.0 / 96, rs, rsk, ps2k)
                P.op("dve", lambda e: e.scalar_tensor_tensor(
                    out=dst[0:96, c0:c0 + n], in0=pst[0:96, 0:n], scalar=gain, in1=rs[0:96, 0:n],
                    op0=ALU.mult, op1=ALU.mult), r=[psk, gk, rsk], w=[(dk, ci)])
                if ci < 4:
                    ps3, ps3k = PS()
                    mm(ps3[0:96, 0:n], RMAT[0:96, 0:96], dst[0:96, c0:c0 + n], True, True,
                       r=["rmat", (dk, ci)], w=[ps3k])
                    t1 = T1[0]
                    t2 = T2[0]
                    t1k = ("m_t1", 0)
                    t2k = ("m_t2", 0)
                    cnt["t"] += 1
                    P.op("pool", lambda e: e.tensor_tensor(out=t1[64:96, 0:n], in0=dst[64:96, c0:c0 + n],
                                                           in1=ROPE[64:96, 0, c0:c0 + n], op=ALU.mult),
                         r=[(dk, ci), "m_rope"], w=[t1k])
                    P.op("dve", lambda e: e.tensor_tensor(out=t2[64:96, 0:n], in0=ps3[64:96, 0:n],
                                                          in1=ROPE[64:96, 1, c0:c0 + n], op=ALU.mult),
                         r=[ps3k, "m_rope"], w=[t2k])
                    P.op("dve", lambda e: e.tensor_tensor(out=dst[64:96, c0:c0 + n], in0=t1[64:96, 0:n],
                                                          in1=t2[64:96, 0:n], op=ALU.add),
                         r=[t1k, t2k], w=[(dk, ci)])

            for h in range(8):
                hp, e2 = h // 2, h % 2
                pb = e2 * 64
                for ci, (c0, n) in enumerate(CHUNKS):
                    pst, psk = PS()
                    for kc in range(2):
                        mm(pst[0:96, 0:n], WKP[:, h, kc, :], KVL[:, kc, c0:c0 + n], kc == 0, False,
                           r=["m_wkp", ("m_kvl", kc, ci)], w=[psk])
                    mm(pst[0:96, 0:n], SHIFT[0:32, :], KR[0:32, c0:c0 + n], False, True,
                       r=["shift", ("m_kr", ci)], w=[psk])
                    head_norm_rope(pst, psk, KH, "m_kh", ci, c0, n, VECS[0:96, l, V_MK:V_MK + 1], "vecs")
                for g in range(3):
                    tts = list(range(g * 8, min(g * 8 + 8, NKT)))
                    pst, psk = PS()
                    for j, tt in enumerate(tts):
                        ci = min(tt // 4, 4)
                        for kc in range(2):
                            mm(pst[:, j * 64:(j + 1) * 64], KVL[:, kc, tt * 128:(tt + 1) * 128],
                               WKVB[:, kc, h * 128 + 64:h * 128 + 128], kc == 0, kc == 1,
                               r=["m_wkvb", ("m_kvl", kc, ci)], w=[psk])
                    nt_ = len(tts)
                    P.op("act", lambda e, pst=pst, nt_=nt_, t0=tts[0], pb=pb: e.copy(
                        out=VH[:, t0:t0 + nt_, pb:pb + 64],
                        in_=pst[:, 0:nt_ * 64].rearrange("p (t d) -> p t d", t=nt_)), r=[psk], w=["m_vh"])
                P.op("pool", lambda e, pb=pb: e.memset(VH[:, :, 64 - pb:128 - pb], 1.0), r=[], w=["m_vh"])
                for ci, (c0, n) in enumerate(CHUNKS):
                    if ci == 4 and not need_ctx:
                        continue
                    pst, psk = PS()
                    for kc in range(3):
                        mm(pst[0:96, 0:n], WQB[:, kc, h * 96:(h + 1) * 96], QL[:, kc, c0:c0 + n], kc == 0, kc == 2,
                           r=["m_wqb", ("m_ql", kc, ci)], w=[psk])
                    head_norm_rope(pst, psk, QH, "m_qh", ci, c0, n, GQ[0:96, l, 1:2], "gq")
                for ci, (c0, n) in enumerate(CHUNKS):
                    if ci == 4 and not need_ctx:
                        continue
                    kts = list(range(NKT)) if ci < 4 else [16, 17]
                    po, pok = PSL()
                    for i, kt in enumerate(kts):
                        pst, psk = PS()
                        mm(pst[:, 0:n], KH[0:96, kt * 128:(kt + 1) * 128], QH[0:96, c0:c0 + n], True, True,
                           r=[("m_kh", min(kt // 4, 4)), ("m_qh", ci)], w=[psk])
                        pt = PT[cnt["pt"] % 2]
                        ptk = ("m_pt", cnt["pt"] % 2)
                        cnt["pt"] += 1
                        act(pt[:, 0:n], pst[:, 0:n], AF.Exp, r=[psk], w=[ptk])
                        mm(po[:, 0:n], VH[:, kt, :], pt[:, 0:n], i == 0, i == len(kts) - 1, r=["m_vh", ptk], w=[pok])
                    rs = RS[cnt["rs"] % 2]
                    rsk = ("m_rs", cnt["rs"] % 2)
                    cnt["rs"] += 1
                    sb = 64 - pb
                    P.op("dve", lambda e, rs=rs, po=po, n=n, pb=pb, sb=sb: e.reciprocal(
                        out=rs[pb:pb + 64, 0:n], in_=po[sb:sb + 64, 0:n]), r=[pok], w=[rsk])
                    P.op("dve", lambda e, rs=rs, po=po, n=n, pb=pb, hp=hp, c0=c0: e.tensor_tensor(
                        out=OM[pb:pb + 64, hp, c0:c0 + n], in0=po[pb:pb + 64, 0:n], in1=rs[pb:pb + 64, 0:n],
                        op=ALU.mult), r=[pok, rsk], w=[("om", hp, ci)])
            P.barrier()
            merge_phase(l, b, 1, OM, "om", need_ctx, Lay(lay_.p, lay_.limit))
            P.barrier()

        def lru_phase(l, b, need_ctx, lay_):
            OL = alloc(lay_, [4, NT], BF16)
            lay2 = Lay(lay_.p, lay_.limit)
            WX_ = [alloc(lay2, [8, 128], BF16) for _ in range(2)]
            WG_ = [alloc(lay2, [8, 128], BF16) for _ in range(2)]
            WLRU = alloc(lay2, [2, 2, 4, 128], BF16)
            lux_off = lay2.take((T + 4 + LC + 4) * 4)
            LUXP = A.view(lux_off, [T + 4], F32)
            LUXC = A.view(lux_off + (T + 4) * 4, [LC + 4], F32)
            H1 = A.view(lux_off, [NT], F32)
            UU = alloc(lay2, [NT], F32)
            UB = alloc(lay2, [NT], BF16)
            AA = alloc(lay2, [NT], F32)
            BB = alloc(lay2, [NT], F32)
            HS = alloc(lay2, [NT], F32)
            TR = [alloc(lay2, [512], F32) for _ in range(2)]
            TI = [alloc(lay2, [512], F32) for _ in range(1)]
            TS = [alloc(lay2, [512], F32) for _ in range(1)]
            P.op("pool", lambda e: e.memset(WLRU, 0.0), w=["wlru"])
            for ax, wd in enumerate((wa_d, wx_d)):
                for d in range(2):
                    for par in range(2):
                        P.op("pool", lambda e, ax=ax, d=d, par=par, wd=wd: e.dma_start(
                            out=WLRU[par * 64:(par + 1) * 64, ax, d, :, par * 64:(par + 1) * 64],
                            in_=wd[l, d, par:8:2].rearrange("n c d -> c n d")), w=["wlru"], dma=True)
            P.op("dve", lambda e: e.memset(LUXP, 0.0), w=["l_luxp"])
            P.op("dve", lambda e: e.memset(LUXC, 0.0), w=["l_luxc"])
            tc_ = [0]
            for j in range(4):
                s = j % 2
                kx, kg = ("l_wx", s), ("l_wg", s)
                wload(WX_[s], wsrc(w_in_d, l, 0, D, C_LX + j * 128, 128), kx)
                wload(WG_[s], wsrc(w_in_d, l, 0, D, C_LG + j * 128, 128), kg)
                if j > 0:
                    P.op("dve", lambda e: e.memset(LUXP[:, 0:2], 0.0), w=["l_luxp"])
                    P.op("dve", lambda e: e.memset(LUXP[:, T + 2:T + 4], 0.0), w=["l_luxp"])
                    P.op("dve", lambda e: e.memset(LUXC[:, 0:2], 0.0), w=["l_luxc"])
                    P.op("dve", lambda e: e.memset(LUXC[:, LC + 2:LC + 4], 0.0), w=["l_luxc"])
                for ci, (c0, n) in enumerate(CHUNKS):
                    pst, psk = PS()
                    for kc in range(8):
                        mm(pst[:, 0:n], WX_[s][:, kc, :], HT[:, kc, c0:c0 + n], kc == 0, kc == 7,
                           r=[kx, ("ht", kc, ci)], w=[psk])
                    if ci < 4:
                        P.op("act", lambda e, pst=pst, c0=c0, n=n: e.copy(out=LUXP[:, 2 + c0:2 + c0 + n],
                                                                          in_=pst[:, 0:n]), r=[psk], w=["l_luxp"])
                    else:
                        P.op("act", lambda e, pst=pst, n=n: e.copy(out=LUXC[:, 2:2 + n], in_=pst[:, 0:n]),
                             r=[psk], w=["l_luxc"])
                for (src, sk, o0, n) in ((LUXP, "l_luxp", 0, T), (LUXC, "l_luxc", T, LC)):
                    P.op("dve", lambda e, src=src, o0=o0, n=n, j=j: e.tensor_scalar(
                        out=UU[:, o0:o0 + n], in0=src[:, 0:n], scalar1=VECS[:, l, V_CW + j:V_CW + j + 1],
                        scalar2=VECS[:, l, V_CB + j:V_CB + j + 1], op0=ALU.mult, op1=ALU.add),
                        r=[sk, "vecs"], w=["l_u"])
                    for jj in range(1, 4):
                        P.op("dve", lambda e, src=src, o0=o0, n=n, j=j, jj=jj: e.scalar_tensor_tensor(
                            out=UU[:, o0:o0 + n], in0=src[:, jj:jj + n],
                            scalar=VECS[:, l, V_CW + jj * 4 + j:V_CW + jj * 4 + j + 1], in1=UU[:, o0:o0 + n],
                            op0=ALU.mult, op1=ALU.add), r=[sk, "vecs", "l_u"], w=["l_u"])
                P.op("pool", lambda e: e.tensor_copy(out=UB, in_=UU), r=["l_u"], w=["l_ub"])
                for d in range(2):
                    for ci, (c0, n) in enumerate(CHUNKS):
                        psr, psrk = PS()
                        mm(psr[:, 0:n], WLRU[:, 0, d, j, :], UB[:, c0:c0 + n], True, True, r=["wlru", "l_ub"], w=[psrk])
                        psi, psik = PS()
                        mm(psi[:, 0:n], WLRU[:, 1, d, j, :], UB[:, c0:c0 + n], True, True, r=["wlru", "l_ub"], w=[psik])
                        k_ = tc_[0] % 2
                        tc_[0] += 1
                        tr, ti, ts = TR[k_], TI[0], TS[0]
                        trk, tik, tsk = ("l_tr", k_), ("l_ti", 0), ("l_ts", 0)
                        act(tr[:, 0:n], psr[:, 0:n], AF.Sigmoid, r=[psrk, "vecs"], w=[trk],
                            bias=VECS[:, l, V_BA + d * 4 + j:V_BA + d * 4 + j + 1])
                        act(ti[:, 0:n], psi[:, 0:n], AF.Sigmoid, r=[psik, "vecs"], w=[tik],
                            bias=VECS[:, l, V_BX + d * 4 + j:V_BX + d * 4 + j + 1])
                        act(AA[:, c0:c0 + n], tr[:, 0:n], AF.Exp, r=[trk, "cdec"], w=[("l_a", ci)],
                            scale=CDEC[:, l, d * 4 + j:d * 4 + j + 1])
                        act(ts[:, 0:n], AA[:, c0:c0 + n], AF.Square, r=[("l_a", ci)], w=[tsk])
                        act(ts[:, 0:n], ts[:, 0:n], AF.Sqrt, r=[tsk, "cst"], w=[tsk], bias=CST[:, 1:2], scale=-1.0)
                        P.op("pool", lambda e, ti=ti, ts=ts, n=n: e.tensor_tensor(out=ti[:, 0:n], in0=ti[:, 0:n],
                                                                               in1=ts[:, 0:n], op=ALU.mult),
                             r=[tik, tsk], w=[tik])
                        P.op("dve", lambda e, ti=ti, c0=c0, n=n: e.tensor_tensor(out=BB[:, c0:c0 + n], in0=ti[:, 0:n],
                                                                                in1=UU[:, c0:c0 + n], op=ALU.mult),
                             r=[tik, "l_u"], w=[("l_b", ci)])
                    HD = HS if d == 0 else H1
                    hk = "l_hs" if d == 0 else "l_h1"
                    hkw = [hk] if d == 0 else [hk, "l_luxp", "l_luxc"]
                    allab = [("l_a", ci) for ci in range(5)] + [("l_b", ci) for ci in range(5)]
                    if d == 0:
                        P.op("dve", lambda e, HD=HD: e.tensor_tensor_scan(
                            out=HD[:, T:NT], data0=AA[:, T:NT], data1=BB[:, T:NT], initial=0.0,
                            op0=ALU.mult, op1=ALU.add), r=allab, w=hkw)
                        P.op("dve", lambda e, HD=HD: e.tensor_tensor_scan(
                            out=HD[:, 0:T], data0=AA[:, 0:T], data1=BB[:, 0:T], initial=HD[:, NT - 1:NT],
                            op0=ALU.mult, op1=ALU.add), r=allab + [hk], w=hkw)
                    else:
                        P.op("dve", lambda e, HD=HD: e.tensor_tensor_scan(
                            out=HD[:, T:NT][:, ::-1], data0=AA[:, T:NT][:, ::-1], data1=BB[:, T:NT][:, ::-1],
                            initial=0.0, op0=ALU.mult, op1=ALU.add), r=allab, w=hkw)
                        P.op("dve", lambda e, HD=HD: e.tensor_tensor_scan(
                            out=HD[:, 0:T][:, ::-1], data0=AA[:, 0:T][:, ::-1], data1=BB[:, 0:T][:, ::-1],
                            initial=HD[:, T:T + 1], op0=ALU.mult, op1=ALU.add), r=allab + [hk], w=hkw)
                        P.op("pool", lambda e: e.tensor_tensor(out=HS, in0=HS, in1=H1, op=ALU.add),
                             r=["l_hs", "l_h1", "l_luxp", "l_luxc"], w=["l_hs"])
                for ci, (c0, n) in enumerate(CHUNKS):
                    if ci == 4 and not need_ctx:
                        continue
                    pst, psk = PS()
                    for kc in range(8):
                        mm(pst[:, 0:n], WG_[s][:, kc, :], HT[:, kc, c0:c0 + n], kc == 0, kc == 7,
                           r=[kg, ("ht", kc, ci)], w=[psk])
                    k_ = tc_[0] % 2
                    tc_[0] += 1
                    tr, ti, ts = TR[k_], TI[0], TS[0]
                    trk, tik, tsk = ("l_tr", k_), ("l_ti", 0), ("l_ts", 0)
                    act(tr[:, 0:n], pst[:, 0:n], AF.Identity, r=[psk], w=[trk])
                    act(ti[:, 0:n], pst[:, 0:n], AF.Square, r=[psk], w=[tik])
                    P.op("dve", lambda e, ti=ti, n=n: e.tensor_scalar(out=ti[:, 0:n], in0=ti[:, 0:n], scalar1=0.044715,
                                                                      scalar2=1.0, op0=ALU.mult, op1=ALU.add),
                         r=[tik], w=[tik])
                    P.op("dve", lambda e, ti=ti, tr=tr, n=n: e.tensor_tensor(out=ti[:, 0:n], in0=ti[:, 0:n],
                                                                           in1=tr[:, 0:n], op=ALU.mult),
                         r=[tik, trk], w=[tik])
                    act(ts[:, 0:n], ti[:, 0:n], AF.Sigmoid, r=[tik], w=[tsk], scale=1.5957691216057308)
                    P.op("pool", lambda e, ts=ts, tr=tr, n=n: e.tensor_tensor(out=ts[:, 0:n], in0=ts[:, 0:n],
                                                                            in1=tr[:, 0:n], op=ALU.mult),
                         r=[tsk, trk], w=[tsk])
                    P.op("dve", lambda e, ts=ts, c0=c0, n=n, j=j: e.tensor_tensor(
                        out=OL[:, j, c0:c0 + n], in0=ts[:, 0:n], in1=HS[:, c0:c0 + n], op=ALU.mult),
                        r=[tsk, "l_hs"], w=[("ol", j, ci)])
            P.barrier()
            merge_phase(l, b, 2, OL, "ol", need_ctx, Lay(lay_.p, lay_.limit))
            P.barrier()

        def ffn_phase(l, b, need_ctx, lay_):
            W1 = [alloc(lay_, [8, 1024], BF16) for _ in range(2)]
            W2 = [alloc(lay_, [8, 1024], BF16) for _ in range(2)]
            A1 = [alloc(lay_, [8, 512], BF16) for _ in range(2)]
            RL = [alloc(lay_, [512], BF16) for _ in range(3)]
            rli = 0
            a1i = 0
            for J in range(4):
                s = J % 2
                k1, k2 = ("f_w1", s), ("f_w2", s)
                wload(W1[s], wsrc(w_ff1_d, l, 0, D, J * 1024, 1024), k1)
                wload(W2[s], wsrc(w_ff2_d, l, J * 1024, 1024, 0, 1024), k2)
                for ci, (c0, n) in enumerate(CHUNKS):
                    if ci == 4 and not need_ctx:
                        continue
                    col = b if ci < 4 else 4
                    a1 = A1[a1i % 2]
                    a1s = a1i % 2
                    a1i += 1
                    for jj in range(8):
                        pst, psk = PS()
                        for kc in range(8):
                            mm(pst[:, 0:n], W1[s][:, kc, jj * 128:(jj + 1) * 128], HT[:, kc, c0:c0 + n],
                               kc == 0, kc == 7, r=[k1, ("ht", kc, ci)], w=[psk])
                        rl = RL[rli % 3]
                        rlk = ("f_rl", rli % 3)
                        rli += 1
                        act(rl[:, 0:n], pst[:, 0:n], AF.Relu, r=[psk], w=[rlk])
                        P.op("pool", lambda e, a1=a1, jj=jj, n=n, rl=rl: e.tensor_tensor(
                            out=a1[:, jj, 0:n], in0=rl[:, 0:n], in1=rl[:, 0:n], op=ALU.mult),
                            r=[rlk], w=[("f_a1", a1s, jj)])
                    for f2 in range(8):
                        psy, psyk = PS()
                        for jj in range(8):
                            mm(psy[:, 0:n], W2[s][:, jj, f2 * 128:(f2 + 1) * 128], a1[:, jj, 0:n], jj == 0, jj == 7,
                               r=[k2, ("f_a1", a1s, jj)], w=[psyk])
                        P.op("dve", lambda e, f2=f2, c0=c0, n=n, psy=psy, col=col: e.scalar_tensor_tensor(
                            out=XT[:, f2, c0:c0 + n], in0=psy[:, 0:n], scalar=MODV[:, l, 5 * 8 + f2, col:col + 1],
                            in1=XT[:, f2, c0:c0 + n], op0=ALU.mult, op1=ALU.add),
                            r=[psyk, "modv", ("xt", f2, ci)], w=[("xt", f2, ci)])

        def load_seq(b, lay_):
            ST = [alloc(lay_, [4, D], F32) for _ in range(2)]
            for g in range(5):
                s = g % 2
                stk = ("ld_st", s)
                if g < 4:
                    src = x_d[b, g * 512:(g + 1) * 512, :].rearrange("(t p) d -> p t d", p=128)
                    nt_ = 4
                else:
                    src = ctx_d[b, :, :].rearrange("(t p) d -> p t d", p=128)
                    nt_ = 2
                P.op("sp", lambda e, s=s, src=src, nt_=nt_: e.dma_start(out=ST[s][:, 0:nt_, :], in_=src),
                     w=[stk], dma=True)
                for f in range(8):
                    pst, psk = PS()
                    for t in range(nt_):
                        P.op("pe", lambda e, pst=pst, t=t, s=s, f=f: e.transpose(
                            out=pst[:, t * 128:(t + 1) * 128], in_=ST[s][:, t, f * 128:(f + 1) * 128], identity=IDF),
                            r=[stk, "idf"], w=[psk])
                    n = nt_ * 128
                    c0 = g * 512
                    if f % 2 == 0:
                        P.op("dve", lambda e, pst=pst, f=f, c0=c0, n=n: e.tensor_copy(out=XT[:, f, c0:c0 + n],
                                                                                    in_=pst[:, 0:n]),
                             r=[psk], w=[("xt", f, g)])
                    else:
                        P.op("act", lambda e, pst=pst, f=f, c0=c0, n=n: e.copy(out=XT[:, f, c0:c0 + n], in_=pst[:, 0:n]),
                             r=[psk], w=[("xt", f, g)])

        def store_seq(b, lay_):
            ST = [alloc(lay_, [D], F32) for _ in range(3)]
            for tt in range(16):
                s = tt % 3
                stk = ("st_st", s)
                ci = tt // 4
                for half in range(2):
                    pst, psk = PS()
                    for fj in range(4):
                        f = half * 4 + fj
                        P.op("pe", lambda e, pst=pst, fj=fj, f=f, tt=tt: e.transpose(
                            out=pst[:, fj * 128:(fj + 1) * 128], in_=XT[:, f, tt * 128:(tt + 1) * 128], identity=IDF),
                            r=[("xt", f, ci), "idf"], w=[psk])
                    if half == 0:
                        P.op("dve", lambda e, pst=pst, s=s: e.tensor_copy(out=ST[s][:, 0:512], in_=pst[:, :]),
                             r=[psk], w=[stk])
                    else:
                        P.op("act", lambda e, pst=pst, s=s: e.copy(out=ST[s][:, 512:1024], in_=pst[:, :]),
                             r=[psk], w=[stk])
                P.op("sp", lambda e, s=s, tt=tt: e.dma_start(out=out_d[b, tt * 128:(tt + 1) * 128, :], in_=ST[s]),
                     r=[stk], dma=True)

        for b in range(nseq):
            load_seq(b, Lay(PH0, PHL))
            P.barrier()
            for l in range(L):
                need_ctx = l < depth_total - 1
                norm_phase(l, 0, b, Lay(PH0, PHL))
                P.barrier()
                na_phase(l, b, need_ctx, Lay(PH0, PHL))
                mla_phase(l, b, need_ctx, Lay(PH0, PHL))
                lru_phase(l, b, need_ctx, Lay(PH0, PHL))
                norm_phase(l, 1, b, Lay(PH0, PHL))
                P.barrier()
                ffn_phase(l, b, need_ctx, Lay(PH0, PHL))
                P.barrier()
            store_seq(b, Lay(PH0, PHL))
            P.barrier()
        P.emit(stack)
    return nc


def _const_tables():
    ident = np.eye(128, dtype=np.float32)
    rmat = np.zeros((128, 96), np.float32)
    for m in range(64, 96):
        i = m - 64
        if i % 16 < 8:
            rmat[m + 8, m] = -1.0
        else:
            rmat[m - 8, m] = 1.0
    shiftm = np.zeros((128, 96), np.float32)
    for i in range(32):
        shiftm[i, 64 + i] = 1.0
    kcol = np.arange(128) % 64
    qcol = np.arange(64)
    cs = np.clip(qcol - 8, 0, GRID_W - 16)
    valid = (kcol[:, None] >= cs[None, :]) & (kcol[:, None] < cs[None, :] + 16)
    cm = np.where(valid, 0.0, NEG).astype(np.float32)
    colmask = np.tile(cm[:, None, :], (1, 16, 1)).reshape(128, 1024)
    t = np.arange(T)
    row = (t // GRID_W).astype(np.float32)
    col = (t % GRID_W).astype(np.float32)
    nf = 8
    inv = (10000.0 ** (-np.arange(nf, dtype=np.float32) / nf)).astype(np.float32)
    ropecs = np.zeros((2, 128, T), np.float32)
    for i in range(32):
        pos = row if i < 16 else col
        ang = (pos * inv[i % 8]).astype(np.float32)
        ropecs[0, 64 + i] = np.cos(ang)
        ropecs[1, 64 + i] = np.sin(ang)
    return dict(ident=ident, rmat=rmat, shiftm=shiftm, colmask=colmask, ropecs=ropecs)


def _pack_vecs(inp, L):
    vecs = np.zeros((L, 128, NV), np.float32)
    for l in range(L):
        v = vecs[l]
        v[:, V_BMOD:V_BMOD + 48] = inp["b_mod"][l].reshape(48, 128).T
        v[:, V_GMIX:V_GMIX + 8] = inp["g_mix"][l].reshape(8, 128).T
        v[:, V_GMLP:V_GMLP + 8] = inp["g_mlp"][l].reshape(8, 128).T
        v[:, V_NAQ] = np.tile(inp["na_q_gain"][l], 2)
        v[:, V_NAK] = np.tile(inp["na_k_gain"][l], 2)
        v[:, V_QA:V_QA + 3] = inp["mla_qa_gain"][l].reshape(3, 128).T
        v[:, V_KVA:V_KVA + 2] = inp["mla_kva_gain"][l].reshape(2, 128).T
        v[0:96, V_MQ] = inp["mla_q_gain"][l]
        v[0:96, V_MK] = inp["mla_k_gain"][l]
        for jj in range(4):
            v[:, V_CW + jj * 4:V_CW + jj * 4 + 4] = inp["lru_conv_w"][l, jj].reshape(4, 128).T
        v[:, V_CB:V_CB + 4] = inp["lru_conv_b"][l].reshape(4, 128).T
        for d in range(2):
            v[:, V_BA + d * 4:V_BA + d * 4 + 4] = inp["lru_ba"][l, d].reshape(4, 128).T
            v[:, V_BX + d * 4:V_BX + d * 4 + 4] = inp["lru_bx"][l, d].reshape(4, 128).T
            v[:, V_LAM + d * 4:V_LAM + d * 4 + 4] = inp["lru_lambda"][l, d].reshape(4, 128).T
    return vecs


def _pack_tb(rpb, L):
    p = np.arange(128)
    i = np.arange(16)
    q = np.arange(64)
    dr = (13 - i)[None, :] + (p[:, None] >= 64)
    ok = (dr >= 0) & (dr <= 14)
    drc = np.clip(dr, 0, 14)
    dc = np.clip((p % 64)[:, None] - q[None, :] + 15, 0, 30)
    tb = rpb[:L][:, :, drc[:, :, None], dc[:, None, :]]
    tb = np.where(ok[None, None, :, :, None], tb, np.float32(0.0))
    return np.ascontiguousarray(tb.reshape(L, 8, 128, 1024)).astype(np.float32)


def host_inputs(inp, core, nseq=BPC, L=DEPTH, shared=None):
    b0 = core * nseq
    m = {}
    m["x"] = np.ascontiguousarray(inp["x"][b0:b0 + nseq])
    m["ctx"] = np.ascontiguousarray(inp["ctx"][b0:b0 + nseq])
    cvecs = np.zeros((5, D), np.float32)
    cvecs[0:nseq] = inp["c"][b0:b0 + nseq]
    cvecs[4] = inp["c_ctx"]
    m["ct"] = np.ascontiguousarray(cvecs.reshape(5, 8, 128).transpose(2, 1, 0)).reshape(128, 40)
    m.update(shared)
    return m


def shared_inputs(inp, L=DEPTH):
    sh = dict(_const_tables())
    sh["vecs"] = _pack_vecs(inp, L)
    sh["tbsrc"] = _pack_tb(np.asarray(inp["na_rpb"]), L)
    for k in ("w_mod", "w_in", "w_q_b", "w_kv_b", "lru_wa", "lru_wx", "w_na_o", "w_mla_o", "w_lru_o", "w_o",
              "w_ff1", "w_ff2"):
        sh[k] = np.ascontiguousarray(np.asarray(inp[k])[:L])
    return sh


_NC_CACHE = {}
SEQ_PER_LAUNCH = 4


def kernel(**inputs):
    inp = {k: np.asarray(v) for k, v in inputs.items()}
    nsl = SEQ_PER_LAUNCH
    if nsl not in _NC_CACHE:
        _NC_CACHE[nsl] = build_program(nseq=nsl)
    nc = _NC_CACHE[nsl]
    sh = shared_inputs(inp)
    B = inp["x"].shape[0]
    out = np.empty((B, T, D), np.float32)
    nlaunch = BPC // nsl
    for k in range(nlaunch):
        in_maps = [host_inputs(inp, k * NCORES + c, nseq=nsl, shared=sh) for c in range(NCORES)]
        res = run_bass_kernel_spmd(nc, in_maps, core_ids=list(range(NCORES)))
        for c in range(NCORES):
            b0 = (k * NCORES + c) * nsl
            out[b0:b0 + nsl] = res.results[c]["out"]
    return out
```
